# Optimizing a Trainium2 kernel written in Bass

```python
import jax, jax.numpy as jnp
from jax import lax
import numpy as np

D_MODEL = 1024
BATCH = 32
SEQ = 256
DEPTH = 4
DEC_BATCH = 4
DEC_SEQ = 4096
PAST_LEN = 512

GRID_W = 64
N_HEADS_A = 8
HEAD_K = 128
HEAD_V = 128
D_A = N_HEADS_A * HEAD_V
N_GROUPS_B = 4
GROUP_B = 128
D_B = N_GROUPS_B * GROUP_B
CHUNK = 32
EPS = 1e-6
SPLITS = (D_A, 2 * D_A, 3 * D_A, 4 * D_A, 5 * D_A, 5 * D_A + D_B, 5 * D_A + 2 * D_B,
          5 * D_A + 2 * D_B + D_MODEL)
D_IN = 5 * D_A + 2 * D_B + 2 * D_MODEL

kernel_name = "hgrn2_fnet_gated_diffusion_step"


def rms_norm(x, w):
    xf = x.astype(jnp.float32)
    y = xf * lax.rsqrt(jnp.mean(xf * xf, axis=-1, keepdims=True) + EPS)
    return (y * w.astype(jnp.float32)).astype(x.dtype)


def layer_lower_bounds(lb_raw):
    p = jax.nn.softmax(lb_raw.astype(jnp.float32), axis=0)
    cs = jnp.cumsum(p, axis=0)
    return cs - cs[0:1]


def grid_pos_embed(length, d):
    rows = length // GRID_W
    r = jnp.repeat(jnp.arange(rows, dtype=jnp.float32), GRID_W)
    col = jnp.tile(jnp.arange(GRID_W, dtype=jnp.float32), rows)
    nf = d // 4
    freqs = 1.0 / (10000.0 ** (jnp.arange(nf, dtype=jnp.float32) / nf))

    def emb(p):
        a = p[:, None] * freqs[None, :]
        return jnp.concatenate([jnp.sin(a), jnp.cos(a)], axis=-1)

    return jnp.concatenate([emb(r), emb(col)], axis=-1)


def hgrn2_chunk_scan(q, log_f, k, v, s0):
    B, L, H, _ = q.shape
    n = L // CHUNK

    def to_chunks(t):
        return t.reshape(B, n, CHUNK, H, t.shape[-1]).transpose(1, 0, 3, 2, 4)

    qc, gc, kc, vc = to_chunks(q), to_chunks(log_f), to_chunks(k), to_chunks(v)
    mask = jnp.tril(jnp.ones((CHUNK, CHUNK), dtype=bool))[:, :, None]

    def step(S, inp):
        qi, gi, ki, vi = inp
        b = jnp.cumsum(gi, axis=-2)
        diff = b[..., :, None, :] - b[..., None, :, :]
        decay = jnp.exp(jnp.where(mask, diff, -jnp.inf))
        att = jnp.einsum('bhtc,bhsc,bhtsc->bhts', qi, ki, decay)
        o = (jnp.einsum('bhts,bhsv->bhtv', att, vi)
             + jnp.einsum('bhtc,bhcv->bhtv', qi * jnp.exp(b), S))
        b_last = b[..., -1:, :]
        S_new = (jnp.exp(b_last[..., 0, :])[..., None] * S
                 + jnp.einsum('bhsc,bhsv->bhcv', ki * jnp.exp(b_last - b), vi))
        return S_new, o

    S_fin, o = lax.scan(step, s0, (qc, gc, kc, vc))
    o = o.transpose(1, 0, 3, 2, 4).reshape(B, L, H, -1)
    return o, S_fin


def forget_gate(fpre, lb):
    lbh = lb.reshape(N_HEADS_A, HEAD_K)
    log_f = jnp.logaddexp(jnp.log(lbh), jnp.log1p(-lbh) + jax.nn.log_sigmoid(fpre))
    return log_f, 1.0 - jnp.exp(log_f)


def mixer(h, lb_f, lb_b, w_in, gnorm_w, w_pa, w_pb, w_o, s0_f, s0_b):
    B, L, _ = h.shape
    proj = h @ w_in
    q, ff, fb, vi, zA, u, zB, gA, gB = jnp.split(proj, SPLITS, axis=-1)

    def heads(t):
        return t.reshape(B, L, N_HEADS_A, -1).astype(jnp.float32)

    qh = jax.nn.silu(heads(q))
    vh = heads(vi)
    logf_f, k_f = forget_gate(heads(ff), lb_f)
    logf_b, k_b = forget_gate(heads(fb), lb_b)
    o_f, S_f = hgrn2_chunk_scan(qh, logf_f, k_f, vh, s0_f.astype(jnp.float32))
    o_b_rev, S_b = hgrn2_chunk_scan(qh[:, ::-1], logf_b[:, ::-1], k_b[:, ::-1], vh[:, ::-1],
                                    s0_b.astype(jnp.float32))
    o = o_f + o_b_rev[:, ::-1]
    o = o * lax.rsqrt(jnp.mean(o * o, axis=-1, keepdims=True) + EPS)
    o = o * gnorm_w.astype(jnp.float32).reshape(N_HEADS_A, HEAD_V)
    yA = (o.reshape(B, L, D_A) * jax.nn.silu(zA.astype(jnp.float32))).astype(h.dtype)

    ug = u.astype(jnp.float32).reshape(B, L, N_GROUPS_B, GROUP_B)
    yF = jnp.fft.fft2(ug, axes=(1, 3), norm='ortho').real.reshape(B, L, D_B)
    yB = (yF * jax.nn.silu(zB.astype(jnp.float32))).astype(h.dtype)

    merged = jax.nn.sigmoid(gA) * (yA @ w_pa) + jax.nn.sigmoid(gB) * (yB @ w_pb)
    return merged @ w_o, S_f, S_b


def setup_inputs(seed: int = 0) -> dict:
    key = jax.random.key(seed)
    ks = jax.random.split(key, 16)
    D = D_MODEL
    f32 = jnp.float32
    return {
        "x_prompt": jax.random.normal(ks[0], (BATCH, SEQ, D), f32),
        "x_sample": jax.random.normal(ks[1], (DEC_BATCH, DEC_SEQ, D), f32),
        "state_hgrn": 0.3 * jax.random.normal(ks[2], (DEC_BATCH, DEPTH, 2, N_HEADS_A, HEAD_K, HEAD_V), f32),
        "c": jax.random.normal(ks[3], (DEC_BATCH, D), f32),
        "c_ctx": jax.random.normal(ks[4], (D,), f32),
        "norm_w": 1.0 + 0.01 * jax.random.normal(ks[5], (DEPTH, D), f32),
        "w_ada": 0.5 * D ** -0.5 * jax.random.normal(ks[6], (DEPTH, D, 3 * D), f32),
        "b_ada": 0.02 * jax.random.normal(ks[7], (DEPTH, 3 * D), f32),
        "w_in": D ** -0.5 * jax.random.normal(ks[8], (DEPTH, D, D_IN), f32),
        "lb_raw": 0.5 * jax.random.normal(ks[9], (DEPTH, 2, D_A), f32),
        "gnorm_w": 1.0 + 0.01 * jax.random.normal(ks[10], (DEPTH, D_A), f32),
        "w_pa": D_A ** -0.5 * jax.random.normal(ks[11], (DEPTH, D_A, D), f32),
        "w_pb": D_B ** -0.5 * jax.random.normal(ks[12], (DEPTH, D_B, D), f32),
        "w_o": D ** -0.5 * jax.random.normal(ks[13], (DEPTH, D, D), f32),
        "final_norm_w": 1.0 + 0.01 * jax.random.normal(ks[14], (D,), f32),
    }


def reference(x_prompt, x_sample, state_hgrn, c, c_ctx, norm_w, w_ada, b_ada, w_in, lb_raw,
              gnorm_w, w_pa, w_pb, w_o, final_norm_w):
    D = D_MODEL
    lbs = layer_lower_bounds(lb_raw)
    xp = x_prompt
    xs = x_sample + grid_pos_embed(x_sample.shape[1], D).astype(x_sample.dtype)[None]
    Bp = xp.shape[0]
    zero_state = jnp.zeros((Bp, N_HEADS_A, HEAD_K, HEAD_V), jnp.float32)
    ctx_states = []
    for l in range(DEPTH):
        mod_p = jax.nn.silu(c_ctx) @ w_ada[l] + b_ada[l]
        sh_p, sc_p, gt_p = mod_p[:D], mod_p[D:2 * D], mod_p[2 * D:]
        hp = rms_norm(xp, norm_w[l]) * (1.0 + sc_p) + sh_p
        out_p, sf, sb = mixer(hp, lbs[l, 0], lbs[l, 1], w_in[l], gnorm_w[l], w_pa[l], w_pb[l],
                              w_o[l], zero_state, zero_state)
        xp = xp + gt_p * out_p
        ctx_states.append(jnp.stack([sf, sb], axis=1).astype(x_prompt.dtype))
        mod_s = jax.nn.silu(c) @ w_ada[l] + b_ada[l]
        sh_s, sc_s, gt_s = mod_s[:, None, :D], mod_s[:, None, D:2 * D], mod_s[:, None, 2 * D:]
        hs = rms_norm(xs, norm_w[l]) * (1.0 + sc_s) + sh_s
        out_s, _, _ = mixer(hs, lbs[l, 0], lbs[l, 1], w_in[l], gnorm_w[l], w_pa[l], w_pb[l],
                            w_o[l], state_hgrn[:, l, 0], state_hgrn[:, l, 1])
        xs = xs + gt_s * out_s
    y_prompt = rms_norm(xp, final_norm_w)
    y_sample = rms_norm(xs, final_norm_w)
    new_state = jnp.stack(ctx_states, axis=1)
    return (y_prompt, y_sample, new_state)
```

```python
import os
import numpy as np
import ml_dtypes
from contextlib import ExitStack
import concourse.bass as bass
import concourse.mybir as mybir
from concourse.bass_utils import run_bass_kernel_spmd

F32 = mybir.dt.float32
BF16 = mybir.dt.bfloat16
AF = mybir.ActivationFunctionType
ALU = mybir.AluOpType
D = 1024
H = 8
EPS = 1e-6
SEG = 256
NSP, NPL = 16, 8


class Prog:
    LAT = 0.15

    def __init__(self, nc, es, plan):
        self.nc, self.plan, self.n = nc, plan, 0
        self.lastw, self.rd = {}, {}
        self.need = set()
        self.ev = {}
        self.isdma, self.engof = [], []
        self.waited = {}
        self.cnt = {'pe': 0, 'act': 0, 'dve': 0, 'pool': 0, 'sp': 0}
        self.E = dict(pe=nc.tensor, act=nc.scalar, dve=nc.vector, pool=nc.gpsimd, sp=nc.sync)
        self.sems = []
        self.semidx = {}
        self.dma_n = {'sp': 0, 'pool': 0}
        self.slot_last = {}
        self.last_real = {}
        self.pending = None
        if plan is not None:
            for e in ('pe', 'act', 'dve', 'pool', 'sp'):
                self.semidx[e] = len(self.sems)
                self.sems.append(es.enter_context(nc.semaphore('s_' + e)))
            for q, n in (('sp', NSP), ('pool', NPL)):
                for i in range(n):
                    self.semidx[(q, i)] = len(self.sems)
                    self.sems.append(es.enter_context(nc.semaphore('d_%s%d' % (q, i))))

    def barrier(self):
        assert self.pending is None
        extra = [i for (i, v) in self.slot_last.values()] + list(self.last_real.values())
        for e in ('pe', 'act', 'dve', 'pool', 'sp'):
            self.op(e, lambda e=e: self.E[e].nop(), [], [], extra=extra, is_nop=True)

    def begin_region(self):
        self.pending = []

    def end_region(self):
        recs, self.pending = self.pending, None
        if self.plan is None or not recs:
            for rec in recs:
                self._emit(rec)
            return
        LAT = float(os.environ.get("KLAT", "0.25"))
        TBL = float(os.environ.get("KTBL", "1.3"))
        SLACK = float(os.environ.get("KSLACK", "0.6"))
        inreg = {rec['i']: rec for rec in recs}
        nleft = {}
        users = {}
        for rec in recs:
            dd = [d for d in rec['deps'] if d in inreg]
            nleft[rec['i']] = len(dd)
            for d in dd:
                users.setdefault(d, []).append(rec['i'])
        blevel = {}
        for rec in reversed(recs):
            i = rec['i']
            c = 2.0 if rec['dma'] else rec['cost']
            m = 0.0
            for u in users.get(i, ()):
                if blevel[u] > m:
                    m = blevel[u]
            blevel[i] = c + m + LAT
        ready = {e: set() for e in self.E}
        for rec in recs:
            if nleft[rec['i']] == 0:
                ready[rec['eng']].add(rec['i'])
        t_eng = {e: 0.0 for e in self.E}
        fin = {}
        fam = [None]
        n_done = 0
        engof = self.engof

        def start_of(i, e):
            st = t_eng[e]
            for d in inreg[i]['deps']:
                f = fin.get(d)
                if f is not None:
                    f += LAT if engof[d] != e else 0.05
                    if f > st:
                        st = f
            if e == 'act':
                fm = inreg[i]['fam']
                if fm is not None and fam[0] is not None and fm != fam[0]:
                    st += TBL
            return st

        while n_done < len(recs):
            best = None
            for e in self.E:
                if not ready[e]:
                    continue
                cands = [(start_of(i, e), i) for i in ready[e]]
                smin = min(c[0] for c in cands)
                pick = None
                for st, i in cands:
                    if st <= smin + SLACK:
                        key = (-blevel[i], i)
                        if pick is None or key < pick[0]:
                            pick = (key, st, i)
                _, st, i = pick
                if best is None or (st, -blevel[i], i) < (best[0], -blevel[best[1]], best[1]):
                    best = (st, i, e)
            st, i, e = best
            ready[e].discard(i)
            rec = inreg[i]
            if e == 'act' and rec['fam'] is not None:
                fam[0] = rec['fam']
            if rec['dma']:
                fin[i] = st + 2.0
                t_eng[e] = st + 0.08
            else:
                fin[i] = st + rec['cost']
                t_eng[e] = st + rec['cost']
            self._emit(rec)
            n_done += 1
            for u in users.get(i, ()):
                nleft[u] -= 1
                if nleft[u] == 0:
                    ready[inreg[u]['eng']].add(u)

    def op(self, eng, fn, r=(), w=(), dma=False, extra=(), is_nop=False, cost=0.3, fam=None):
        i = self.n
        self.n += 1
        deps = set(extra)
        for k in r:
            if k in self.lastw:
                deps.add(self.lastw[k])
        for k in w:
            if k in self.lastw:
                deps.add(self.lastw[k])
            rr = self.rd.get(k)
            if rr:
                deps.update(rr.values())
        self.isdma.append(dma)
        self.engof.append(eng)
        region = self.pending is not None
        for k in r:
            key = ('dma', i) if (dma or region) else eng
            self.rd.setdefault(k, {})[key] = i
        for k in w:
            self.lastw[k] = i
            self.rd[k] = {}
        rec = dict(i=i, eng=eng, fn=fn, deps=deps, dma=dma, is_nop=is_nop, cost=cost, fam=fam)
        real = [d for d in deps if self.isdma[d] or dma or self.engof[d] != eng or eng != 'pe']
        if self.plan is None:
            self.need.update(real)
        if region:
            self.pending.append(rec)
        else:
            self._emit(rec)
        return i

    def _emit(self, rec):
        i, eng, dma = rec['i'], rec['eng'], rec['dma']
        if not dma and not rec['is_nop']:
            self.last_real[eng] = i
        deps = set(rec['deps'])
        slot = prev = None
        if dma:
            slot = self.dma_n[eng] % (NSP if eng == 'sp' else NPL)
            self.dma_n[eng] += 1
            prev = self.slot_last.get((eng, slot))
            if prev is not None:
                deps.add(prev[0])
        if self.plan is None:
            if dma:
                self.slot_last[(eng, slot)] = (i, 0)
            return
        real = [d for d in deps if self.isdma[d] or dma or self.engof[d] != eng or eng != 'pe']
        E = self.E[eng]
        for d in real:
            s, v = self.ev[d]
            if self.waited.get((eng, s), 0) < v:
                E.wait_ge(self.sems[s], v)
                self.waited[(eng, s)] = v
        inst = rec['fn']()
        if dma:
            s = self.semidx[(eng, slot)]
            v = (prev[1] if prev else 0) + 16
            inst.then_inc(self.sems[s], 16)
            self.ev[i] = (s, v)
            self.slot_last[(eng, slot)] = (i, v)
        elif i in self.plan:
            self.cnt[eng] += 1
            s = self.semidx[eng]
            inst.then_inc(self.sems[s], 1)
            self.ev[i] = (s, self.cnt[eng])

    def finish(self):
        if self.plan is None:
            return
        assert self.pending is None
        for (q, slot), (i, v) in self.slot_last.items():
            self.nc.sync.wait_ge(self.sems[self.semidx[(q, slot)]], v)


import os
STOP = os.environ.get("KSTOP", "")
SCHED_DEFAULT = {"KS_ADA": "1", "KS_A": "1", "KS_C": "1", "KS_D": "1"}


def build(nc, P, NT, DEPTH, NSO):
    try:
        _build(nc, P, NT, DEPTH, NSO)
    except StopIteration:
        if P.pending is not None:
            P.end_region()
    P.finish()


def _build(nc, P, NT, DEPTH, NSO):
    NTOK = NT * 128
    NB = NT // 4
    NCH = NT * 4
    es = ExitStack()

    def din(name, shape, dt=F32):
        return nc.dram_tensor(name, list(shape), dt, kind="ExternalInput").ap()

    x_d = din("x", [NTOK, D])
    pos_d = din("pos", [NTOK, D])
    cv_d = din("cv", [128, 8])
    s0_d = din("s0", [DEPTH * 2 * H, 128, 128])
    keep_d = din("keep", [128, 1])
    nw_d = din("nw", [128, DEPTH * 8])
    wada_d = din("w_ada", [DEPTH, D, 3 * D])
    badap_d = din("bada_p", [128, DEPTH * 16])
    badag_d = din("bada_g", [DEPTH, 128, D])
    win_d = din("w_in", [DEPTH, D, 8192])
    lbr_d = din("lbr", [128, DEPTH * 16])
    gnw_d = din("gnw", [128, DEPTH * 8])
    wpa_d = din("w_pa", [DEPTH, D, D])
    wpb_d = din("w_pb", [DEPTH, 512, D])
    wo_d = din("w_o", [DEPTH, D, D])
    fnw_d = din("fnw", [128, D])
    ident_d = din("ident", [128, 128], BF16)
    mask_d = din("mask", [128, 512])
    smask_d = din("smask", [128, 512])
    cs_d = din("cs", [128, 256], BF16)
    tbl_d = din("tbl", [NB, NT, 128, 1024], BF16)
    y_d = nc.dram_tensor("y", [NTOK, D], F32, kind="ExternalOutput").ap()
    st_d = nc.dram_tensor("st", [NSO * DEPTH * 2 * H, 128, 128], F32, kind="ExternalOutput").ap()
    xcur_d = nc.dram_tensor("xcur", [NTOK, D], F32).ap()
    ya_d = nc.dram_tensor("ya_s", [H, 128, NTOK], BF16).ap()
    yb_d = nc.dram_tensor("yb_s", [4, 128, NTOK], BF16).ap()

    DBG = os.environ.get("KDBG", "") == "1"
    if DBG:
        dbg_d = nc.dram_tensor("dbg", [40, 128, 512], F32, kind="ExternalOutput").ap()

    dscr_n = [0]

    def dbg(slot, ap, key, ncols=512, bf=False):
        if DBG and ncols < 2:
            t_ = dscr_t[dscr_n[0]]
            dscr_n[0] += 1
            k_ = 'dscr%d' % dscr_n[0]
            P.op('pool', lambda: nc.gpsimd.memset(t_[:], 0.0), [], [k_])
            P.op('pool', lambda: nc.gpsimd.tensor_copy(out=t_[:, 0:1], in_=ap), [key, k_], [k_])
            ap, key, ncols = t_[:], k_, 2
        if DBG:
            q = 'pool' if bf else 'sp'
            eng = nc.gpsimd if bf else nc.sync
            P.op(q, lambda: eng.dma_start(out=dbg_d[slot][:, 0:ncols], in_=ap), [key], [], dma=True)

    uid = [0]

    def sb(name, shape, dt=F32, st=es):
        uid[0] += 1
        return st.enter_context(nc.sbuf_tensor("s%d_%s" % (uid[0], name), list(shape), dt))

    PB = [es.enter_context(nc.psum_tensor("pb%d" % i, [128, 512], F32)) for i in range(7)]
    PB.append(es.enter_context(nc.psum_tensor("pb7", [128, 1024], BF16)))
    BK = ["B%d" % i for i in range(8)]

    dscr_t = [sb("dscr%d" % i, [128, 2]) for i in range(8)] if DBG else []
    hT = sb("hT", [128, 8, NTOK], BF16)
    ident = sb("ident", [128, 128], BF16)
    mask = sb("mask", [128, 512])
    smask = sb("smask", [128, 512])
    cs = sb("cs", [128, 256], BF16)
    onesf = sb("onesf", [128, 128])
    epsc = sb("epsc", [128, 1])
    keep = sb("keep", [128, 1])
    cv = sb("cv", [128, 8])
    cvs = sb("cvs", [128, 8])
    scB = sb("scB", [128, 8, 128], BF16)
    cvb = sb("cvb", [128, 8], BF16)
    onesb = sb("onesb", [128, 128], BF16)
    nw = sb("nw", [128, DEPTH * 8])
    badap = sb("badap", [128, DEPTH * 16])
    lbr = sb("lbr", [128, DEPTH * 16])
    lbe = sb("lbe", [128, DEPTH * 16])
    lbt = sb("lbt", [128, DEPTH * 16])
    lbs = sb("lbs", [128, 16])
    gnw = sb("gnw", [128, DEPTH * 8])
    shsc = sb("shsc", [128, 16])
    sc1p = sb("sc1p", [128, 8])
    gtB = sb("gtB", [128, D])

    def fsz(ap):
        n = 1
        for x in ap.shape[1:]:
            n *= int(x)
        return n

    def dma(q, out, in_, r, w):
        eng = nc.sync if q == 'sp' else nc.gpsimd
        P.op(q, lambda: eng.dma_start(out=out, in_=in_), r, w, dma=True)

    def cdma(out, in_, r, w, b=512):
        n = out.shape[-1]
        if len(out.shape) == 2 and n > b:
            out = out.rearrange("p (a b) -> p a b", b=b)
            in_ = in_.rearrange("p (a b) -> p a b", b=b)
        dma('pool', out, in_, r, w)

    def mm(out, lhsT, rhs, start, stop, r, w, **kw):
        P.op('pe', lambda: nc.tensor.matmul(out, lhsT=lhsT, rhs=rhs, start=start, stop=stop, **kw), r, w,
             cost=0.04 + fsz(rhs) / 2000.0)

    def tr(out, in_, r, w):
        P.op('pe', lambda: nc.tensor.transpose(out=out, in_=in_, identity=ident[:]), r, w, cost=0.1)

    FAM = {AF.Sigmoid: 'sig', AF.Silu: 'silu', AF.Ln: 'ln', AF.Exp: 'ln', AF.Sqrt: 'sqrt'}

    def act(out, in_, func, r, w, **kw):
        P.op('act', lambda: nc.scalar.activation(out=out, in_=in_, func=func, **kw), r, w,
             cost=0.2 + fsz(out) / 1400.0, fam=FAM.get(func))

    def tt(e, out, in0, in1, op, r, w):
        eng = nc.vector if e == 'dve' else nc.gpsimd
        P.op(e, lambda: eng.tensor_tensor(out=out, in0=in0, in1=in1, op=op), r, w,
             cost=(0.08 + fsz(out) / 960.0) if e == 'dve' else (0.5 + fsz(out) / 400.0))

    def ts(e, out, in0, s1, s2, op0, op1, r, w):
        eng = nc.vector if e == 'dve' else nc.gpsimd
        c_ = (0.08 + fsz(out) / 960.0) if e == 'dve' else (0.5 + fsz(out) / 400.0)
        if op1 is None:
            P.op(e, lambda: eng.tensor_scalar(out=out, in0=in0, scalar1=s1, scalar2=None, op0=op0), r, w, cost=c_)
        else:
            P.op(e, lambda: eng.tensor_scalar(out=out, in0=in0, scalar1=s1, scalar2=s2, op0=op0, op1=op1), r, w, cost=c_)

    def stt(out, in0, scalar, in1, op0, op1, r, w):
        P.op('dve', lambda: nc.vector.scalar_tensor_tensor(out=out, in0=in0, scalar=scalar, in1=in1,
                                                           op0=op0, op1=op1), r, w, cost=0.08 + fsz(out) / 960.0 + 0.1)

    def scan(out, d0, d1, r, w):
        P.op('dve', lambda: nc.vector.tensor_tensor_scan(out=out, data0=d0, data1=d1, initial=0.0,
                                                         op0=ALU.mult, op1=ALU.add), r, w, cost=0.08 + fsz(out) / 480.0)

    def rsum(out, in_, r, w):
        P.op('dve', lambda: nc.vector.reduce_sum(out=out, in_=in_, axis=mybir.AxisListType.X), r, w,
             cost=0.08 + fsz(in_) / 960.0)

    def recip(out, in_, r, w):
        P.op('dve', lambda: nc.vector.reciprocal(out=out, in_=in_), r, w, cost=0.08 + fsz(out) / 120.0)

    def cp(e, out, in_, r, w):
        if e == 'act':
            P.op('act', lambda: nc.scalar.activation(out=out, in_=in_, func=AF.Identity), r, w,
                 cost=0.2 + fsz(out) / 1400.0)
        else:
            eng = nc.vector if e == 'dve' else nc.gpsimd
            P.op(e, lambda: eng.tensor_copy(out=out, in_=in_), r, w,
                 cost=(0.08 + fsz(out) / 960.0) if e == 'dve' else (0.5 + fsz(out) / 400.0))

    for t_, d_, k_ in ((ident, ident_d, 'ident'), (mask, mask_d, 'mask'), (smask, smask_d, 'smask'),
                       (cs, cs_d, 'cs'), (keep, keep_d, 'keep'), (cv, cv_d, 'cv'), (nw, nw_d, 'nw'),
                       (badap, badap_d, 'badap'), (lbr, lbr_d, 'lbr'), (gnw, gnw_d, 'gnw')):
        dma('sp', t_[:], d_, [], [k_])
    P.op('dve', lambda: nc.vector.memset(onesf[:], 1.0), [], ['onesf'])
    P.op('dve', lambda: nc.vector.memset(epsc[:], EPS), [], ['epsc'])
    act(cvs[:], cv[:], AF.Silu, ['cv'], ['cvs'])
    act(cvb[:], cv[:], AF.Silu, ['cv'], ['cvb'])
    P.op('pool', lambda: nc.gpsimd.memset(onesb[:], 1.0), [], ['onesb'])
    for kc in range(8):
        ts('dve', scB[:, kc, :], onesf[:], cvs[:, kc:kc + 1], None, ALU.mult, None, ['onesf', 'cvs'], ['scB'])
    act(lbe[:], lbr[:], AF.Exp, ['lbr'], ['lbe'])
    cp('dve', lbs[:], lbe[:, 0:16], ['lbe'], ['lbs'])
    for l in range(1, DEPTH):
        tt('dve', lbs[:], lbs[:], lbe[:, l * 16:(l + 1) * 16], ALU.add, ['lbs', 'lbe'], ['lbs'])
    P.op('dve', lambda: nc.vector.reciprocal(out=lbs[:], in_=lbs[:]), ['lbs'], ['lbs'])
    P.op('dve', lambda: nc.vector.memset(lbt[:, 0:16], 0.0), [], ['lbt'])
    for l in range(1, DEPTH):
        tt('dve', lbe[:, l * 16:(l + 1) * 16], lbe[:, l * 16:(l + 1) * 16], lbs[:], ALU.mult, ['lbe', 'lbs'], ['lbe'])
        tt('dve', lbt[:, l * 16:(l + 1) * 16], lbt[:, (l - 1) * 16:l * 16], lbe[:, l * 16:(l + 1) * 16], ALU.add,
           ['lbt', 'lbe'], ['lbt'])

    for l in range(DEPTH):
        last = (l == DEPTH - 1)
        with ExitStack() as ps:
            wa = [sb("wa%d" % i, [128, 3 * D], BF16, st=ps) for i in range(2)]
            gb = sb("gb", [128, D], st=ps)
            if os.environ.get("KS_ADA", SCHED_DEFAULT["KS_ADA"]) == "1":
                P.begin_region()
            dma('sp', gb[:], badag_d[l], [], ['gb'])
            for kc in range(8):
                wt = wa[kc % 2]
                wk = 'wa%d' % (kc % 2)
                cdma(wt[:], wada_d[l][kc * 128:(kc + 1) * 128, :], [], [wk])
                for j in range(16):
                    mm(PB[0][:, j:j + 1], wt[:, j * 128:(j + 1) * 128], cvb[:, kc:kc + 1],
                       (kc == 0 and j == 0), (kc == 7 and j == 15), [wk, 'cvb'], [BK[0]])
                for b in range(2):
                    mm(PB[1 + b][:, :], scB[:, kc, :], wt[:, 2048 + b * 512:2048 + (b + 1) * 512],
                       kc == 0, kc == 7, [wk, 'scB'], [BK[1 + b]])
            tt('dve', shsc[:], PB[0][:, 0:16], badap[:, l * 16:(l + 1) * 16], ALU.add, [BK[0], 'badap'], ['shsc'])
            stt(sc1p[:], shsc[:, 8:16], 1.0, nw[:, l * 8:(l + 1) * 8], ALU.add, ALU.mult, ['shsc', 'nw'], ['sc1p'])
            for b in range(2):
                tt('dve', gtB[:, b * 512:(b + 1) * 512], PB[1 + b][:, :], gb[:, b * 512:(b + 1) * 512], ALU.add,
                   [BK[1 + b], 'gb'], ['gtB'])
            if P.pending is not None:
                P.end_region()

        P.barrier()
        if STOP == "ada":
            raise StopIteration
        with ExitStack() as ps:
            xt = [sb("xt%d" % i, [128, D], st=ps) for i in range(2)]
            pt = sb("pt", [128, D], st=ps)
            xn = [sb("xn%d" % i, [128, D], BF16, st=ps) for i in range(2)]
            junk2 = [sb("junk%d" % i, [128, D], st=ps) for i in range(2)]
            ss2 = [sb("ss%d" % i, [128, 1], st=ps) for i in range(2)]
            ms2 = [sb("ms%d" % i, [128, 1], st=ps) for i in range(2)]
            rstd2 = [sb("rstd%d" % i, [128, 1], st=ps) for i in range(2)]
            if os.environ.get("KS_A", SCHED_DEFAULT["KS_A"]) == "1":
                P.begin_region()
            for i in range(NT):
                X, xk = xt[i % 2], 'xt%d' % (i % 2)
                XN, xnk = xn[i % 2], 'xn%d' % (i % 2)
                rows = slice(i * 128, (i + 1) * 128)
                if l == 0:
                    dma('sp', X[:], x_d[rows, :], [], [xk])
                    dma('sp', pt[:], pos_d[rows, :], [], ['pt'])
                    tt('dve', X[:], X[:], pt[:], ALU.add, [xk, 'pt'], [xk])
                    dma('sp', xcur_d[rows, :], X[:], [xk], ['xcur%d' % i])
                else:
                    dma('sp', X[:], xcur_d[rows, :], ['xcur%d' % i], [xk])
                b_ = i % 2
                junk, ss, ms, rstd = junk2[b_], ss2[b_], ms2[b_], rstd2[b_]
                jk, sk, mk, rk = 'junk%d' % b_, 'ss%d' % b_, 'ms%d' % b_, 'rstd%d' % b_
                act(junk[:], X[:], AF.Square, [xk], [jk])
                rsum(ss[:], junk[:], [jk], [sk])
                ts('dve', ms[:], ss[:], 1.0 / D, EPS, ALU.mult, ALU.add, [sk], [mk])
                act(ms[:], ms[:], AF.Sqrt, [mk], [mk])
                recip(rstd[:], ms[:], [mk], [rk])
                ts('dve', XN[:], X[:], rstd[:, 0:1], None, ALU.mult, None, [xk, rk], [xnk])
                if l == 1 and i >= NT - 2:
                    dbg(36 + (i - NT + 2), X[:, 0:512], xk)
                for kc in range(8):
                    tr(PB[7][:, kc * 128:(kc + 1) * 128], XN[:, kc * 128:(kc + 1) * 128], [xnk, 'ident'], [BK[7]])
                for kc in range(8):
                    o_ = hT[:, kc, i * 128:(i + 1) * 128]
                    i_ = PB[7][:, kc * 128:(kc + 1) * 128]
                    act(o_, i_, AF.Identity, [BK[7], 'sc1p', 'shsc'], ['hT'],
                        bias=shsc[:, kc:kc + 1], scale=sc1p[:, kc:kc + 1])

        if P.pending is not None:
            P.end_region()
        if STOP == "A":
            dbg(8, hT[:, 0, 0:512], 'hT', bf=True)
            dbg(0, hT[:, 1, 0:512], 'hT', bf=True)
        P.barrier()
        if STOP == "A":
            raise StopIteration
        with ExitStack() as ps:
            wh = [sb("wh%d" % i, [128, 8, 5, 128], BF16, st=ps) for i in range(2)]
            qbT = sb("qbT", [128, NTOK], BF16, st=ps)
            khb = sb("khb", [128, NT, 128], BF16, st=ps)
            vall = sb("vall", [128, NT, 128], BF16, st=ps)
            oacc = sb("oacc", [128, NTOK], st=ps)
            Dall = sb("Dall", [128, 2, NCH], st=ps)
            qs = sb("qs", [128, 512], st=ps)
            sg = [sb("sg%d" % i, [128, 512], st=ps) for i in range(2)]
            lf = [sb("lf%d" % i, [128, 512], st=ps) for i in range(2)]
            bc = [sb("bc%d" % i, [128, 512], st=ps) for i in range(2)]
            bb = sb("bb", [128, 512], st=ps)
            ea = [sb("ea%d" % i, [128, 512], st=ps) for i in range(2)]
            en = [sb("en%d" % i, [128, 512], st=ps) for i in range(2)]
            lbm = sb("lbm", [128, DEPTH * 16], st=ps)
            ts('dve', lbm[:], lbt[:], -1.0, 1.0, ALU.mult, ALU.add, ['lbt'], ['lbm'])
            er = [sb("er%d" % i, [128, 16], st=ps) for i in range(2)]
            ee = [sb("ee%d" % i, [128, 16], st=ps) for i in range(2)]
            qfT = [sb("qfT%d" % p, [128, 512], BF16, st=ps) for p in range(2)]
            q1 = [[sb("q1%d%d" % (p, i), [128, 512], BF16, st=ps) for i in range(2)] for p in range(2)]
            k1 = [[sb("k1%d%d" % (p, i), [128, 512], BF16, st=ps) for i in range(2)] for p in range(2)]
            q2 = [[sb("q2%d%d" % (p, i), [128, 512], BF16, st=ps) for i in range(2)] for p in range(2)]
            k2 = [[sb("k2%d%d" % (p, i), [128, 512], BF16, st=ps) for i in range(2)] for p in range(2)]
            khT = [sb("khT%d" % i, [128, 512], BF16, st=ps) for i in range(2)]
            khf = [sb("khf%d" % p, [128, 4, 128], BF16, st=ps) for p in range(2)]
            vblk = [sb("vblk%d" % p, [128, 4, 4, 128], BF16, st=ps) for p in range(2)]
            am = [sb("am%d" % p, [128, 512], BF16, st=ps) for p in range(2)]
            NS = 3
            Sf = [[sb("Sf%d%d" % (i, j), [128, 128], st=ps) for j in range(NS)] for i in range(2)]
            Sb = [[sb("Sb%d%d" % (i, j), [128, 128], BF16, st=ps) for j in range(NS)] for i in range(2)]
            szb = sb("szb", [128, 512], BF16, st=ps)
            ot2 = [sb("ot%d" % i, [128, 128], st=ps) for i in range(2)]
            osq2 = [sb("osq%d" % i, [128, 128], BF16, st=ps) for i in range(2)]
            lnm2 = [sb("lnm%d" % i, [128, 128], st=ps) for i in range(2)]
            rs22 = [sb("rs2%d" % i, [128, 128], st=ps) for i in range(2)]
            t12 = [sb("t1%d" % i, [128, 128], st=ps) for i in range(2)]
            yst = [sb("yst%d" % i, [128, 128], BF16, st=ps) for i in range(2)]
            for p in range(2):
                for i in range(2):
                    P.op('pool', lambda: nc.gpsimd.memset(q2[p][i][:], 0.0), [], ['q2%d%d' % (p, i)])
                    P.op('pool', lambda: nc.gpsimd.memset(k2[p][i][:], 0.0), [], ['k2%d%d' % (p, i)])
                P.op('pool', lambda: nc.gpsimd.memset(vblk[p][:], 0.0), [], ['vblk%d' % p])

            def load_wh(h):
                W, wk = wh[h % 2], 'wh%d' % (h % 2)
                for si in range(5):
                    c0 = si * 1024 + h * 128
                    dma('pool', W[:, :, si, :],
                        win_d[l][:, c0:c0 + 128].rearrange("(kc p) c -> p kc c", p=128), [], [wk])

            def v4(ap):
                return ap.rearrange("p (c z t) -> p c z t", z=2, t=16)

            if os.environ.get("KSCHED", "1") == "1":
                P.begin_region()
            load_wh(0)
            for h in range(H):
                W, wk = wh[h % 2], 'wh%d' % (h % 2)
                if h + 1 < H:
                    load_wh(h + 1)
                cur = [0, 0]
                for d in range(2):
                    dma('sp', Sf[d][0][:], s0_d[(l * 2 + d) * H + h], [], ['Sf%d0' % d])
                    cp('act', Sb[d][0][:], Sf[d][0][:], ['Sf%d0' % d], ['Sb%d0' % d])

                def step_state(d, ch, ubank_ap, ubk, seg_end):
                    c = cur[d]
                    n = (c + 1) % NS
                    stt(Sf[d][n][:], Sf[d][c][:], Dall[:, d, ch:ch + 1], ubank_ap, ALU.mult, ALU.add,
                        ['Sf%d%d' % (d, c), 'Dall', ubk], ['Sf%d%d' % (d, n)])
                    if seg_end:
                        seg = ch // 8
                        if seg < NSO:
                            dma('sp', st_d[((seg * DEPTH + l) * 2 + d) * H + h], Sf[d][n][:], ['Sf%d%d' % (d, n)], [])
                        n2 = (n + 1) % NS
                        ts('dve', Sf[d][n2][:], Sf[d][n][:], keep[:, 0:1], None, ALU.mult, None,
                           ['Sf%d%d' % (d, n), 'keep'], ['Sf%d%d' % (d, n2)])
                        n = n2
                    cur[d] = n
                    cp('act', Sb[d][n][:], Sf[d][n][:], ['Sf%d%d' % (d, n)], ['Sb%d%d' % (d, n)])

                def stage_proj(blk):
                    cols = slice(blk * 512, (blk + 1) * 512)
                    for si, bk in ((0, 0), (1, 1), (2, 2)):
                        for kc in range(8):
                            mm(PB[bk][:, :], W[:, kc, si, :], hT[:, kc, cols], kc == 0, kc == 7, [wk, 'hT'], [BK[bk]])
                    for t in range(4):
                        tcg = slice(blk * 512 + t * 128, blk * 512 + (t + 1) * 128)
                        for kc in range(8):
                            mm(PB[4][:, t * 128:(t + 1) * 128], hT[:, kc, tcg], W[:, kc, 3, :], kc == 0, kc == 7,
                               [wk, 'hT'], [BK[4]])

                def elem_dir(blk, d):
                    p = blk % 2
                    cols = slice(blk * 512, (blk + 1) * 512)
                    li = l * 16 + d * 8 + h
                    SG, sgk, LF, lfk, BC, bck = sg[d], 'sg%d' % d, lf[d], 'lf%d' % d, bc[d], 'bc%d' % d
                    EA, eak, EN, enk, ER, erk, EE, eek = ea[d], 'ea%d' % d, en[d], 'en%d' % d, er[d], 'er%d' % d, ee[d], 'ee%d' % d
                    act(SG[:], PB[1 + d][:, :], AF.Sigmoid, [BK[1 + d]], [sgk])
                    yield
                    ts('dve', SG[:], SG[:], lbm[:, li:li + 1], lbt[:, li:li + 1], ALU.mult, ALU.add, [sgk, 'lbm', 'lbt'], [sgk])
                    yield
                    act(LF[:], SG[:], AF.Ln, [sgk], [lfk])
                    yield
                    act(SG[:], SG[:], AF.Identity, [sgk], [sgk], bias=1.0, scale=-1.0)
                    scan(BC[:], smask[:], LF[:], ['smask', lfk], [bck])
                    yield
                    if d == 0:
                        B_, bk_ = BC, bck
                        zn, zf, rpos, epos = 0, 1, 15, 15
                    else:
                        b3 = BC[:].rearrange("p (c t) -> p c t", t=16)
                        bb3 = bb[:].rearrange("p (c t) -> p c t", t=16)
                        tt('dve', bb3, b3[:, :, 15:16].to_broadcast([128, 32, 16]), b3, ALU.subtract, [bck], ['bb'])
                        tt('dve', bb[:], bb[:], LF[:], ALU.add, ['bb', lfk], ['bb'])
                        B_, bk_ = bb, 'bb'
                        zn, zf, rpos, epos = 1, 0, 0, 0
                        yield
                    B4 = v4(B_[:])
                    act(EA[:], B_[:], AF.Exp, [bk_], [eak])
                    act(EN[:], B_[:], AF.Exp, [bk_], [enk], scale=-1.0)
                    act(ER[:], B4[:, :, zn, rpos], AF.Exp, [bk_], [erk])
                    act(EE[:], B4[:, :, zf, epos], AF.Exp, [bk_], [eek])
                    yield
                    tt('dve', EA[:], qs[:], EA[:], ALU.mult, ['qs', eak], [eak])
                    tt('dve', EN[:], SG[:], EN[:], ALU.mult, [sgk, enk], [enk])
                    tt('dve', Dall[:, d, blk * 16:(blk + 1) * 16], ER[:], EE[:], ALU.mult, [erk, eek], ['Dall'])
                    yield
                    cp('act', q1[p][d][:], EA[:], [eak], ['q1%d%d' % (p, d)])
                    cp('act', k1[p][d][:], EN[:], [enk], ['k1%d%d' % (p, d)])
                    er_b = ER[:].unsqueeze(2).to_broadcast([128, 16, 16])
                    ee_b = EE[:].unsqueeze(2).to_broadcast([128, 16, 16])
                    D_b = Dall[:, d, blk * 16:(blk + 1) * 16].unsqueeze(2).to_broadcast([128, 16, 16])
                    q1f4, k1f4 = v4(EA[:]), v4(EN[:])
                    if d == 0:
                        QF4, qfk = v4(qfT[p][:]), 'qfT%d' % p
                    else:
                        QF4, qfk = v4(qbT[:, cols]), 'qbT'
                    KH4 = v4(khT[d][:])
                    tt('dve', KH4[:, :, zn, :], k1f4[:, :, zn, :], D_b, ALU.mult, [enk, 'Dall'], ['khT%d' % d])
                    tt('dve', KH4[:, :, zf, :], k1f4[:, :, zf, :], ee_b, ALU.mult, [enk, eek], ['khT%d' % d])
                    yield
                    for t in range(4):
                        tr(PB[7][:, (d * 4 + t) * 128:(d * 4 + t + 1) * 128], khT[d][:, t * 128:(t + 1) * 128],
                           ['khT%d' % d, 'ident'], [BK[7]])
                    tt('dve', QF4[:, :, zf, :], q1f4[:, :, zf, :], er_b, ALU.mult, [eak, erk], [qfk])
                    cp('act', QF4[:, :, zn, :], q1f4[:, :, zn, :], [eak], [qfk])
                    yield
                    cp('act', v4(q2[p][d][:])[:, :, zf, :], q1f4[:, :, zf, :], [eak], ['q2%d%d' % (p, d)])
                    tt('dve', v4(k2[p][d][:])[:, :, zn, :], k1f4[:, :, zn, :], er_b, ALU.mult, [enk, erk],
                       ['k2%d%d' % (p, d)])
                    yield

                def stage_elem(blk):
                    p = blk % 2
                    act(qs[:], PB[0][:, :], AF.Sigmoid, [BK[0]], ['qs'])
                    cp('dve', vall[:, blk * 4:(blk + 1) * 4, :], PB[4][:, :].rearrange("p (t v) -> p t v", t=4),
                       [BK[4]], ['vall'])
                    tt('dve', qs[:], PB[0][:, :], qs[:], ALU.mult, [BK[0], 'qs'], ['qs'])
                    gens = [elem_dir(blk, 0), elem_dir(blk, 1)]
                    alive = [True, True]
                    n_ = 0
                    while any(alive):
                        for d in range(2):
                            if alive[d]:
                                try:
                                    next(gens[d])
                                except StopIteration:
                                    alive[d] = False
                        n_ += 1
                        if n_ == 1:
                            if blk + 1 < NB:
                                stage_proj(blk + 1)
                            for j in range(4):
                                dma('sp', vblk[p][32 * j:32 * j + 32, :, j, :],
                                    vall[32 * j:32 * j + 32, blk * 4:(blk + 1) * 4, :], ['vall'], ['vblk%d' % p])
                        if n_ % 2 == 0:
                            yield
                    for t in range(4):
                        cp('act', khf[p][:, t, :], PB[7][:, t * 128:(t + 1) * 128], [BK[7]], ['khf%d' % p])
                        cp('act', khb[:, blk * 4 + t, :], PB[7][:, 512 + t * 128:512 + (t + 1) * 128], [BK[7]], ['khb'])

                def stage_scan(blk):
                    p = blk % 2
                    for t in range(4):
                        tile = blk * 4 + t
                        tcl = slice(t * 128, (t + 1) * 128)
                        tcg = slice(tile * 128, (tile + 1) * 128)
                        A, ak = am[tile % 2], 'am%d' % (tile % 2)
                        for d in range(2):
                            mm(PB[5][:, (2 * d) * 128:(2 * d + 1) * 128], k1[p][d][:, tcl], q1[p][d][:, tcl], True, True,
                               ['k1%d%d' % (p, d), 'q1%d%d' % (p, d)], [BK[5]])
                            mm(PB[5][:, (2 * d + 1) * 128:(2 * d + 2) * 128], k2[p][d][:, tcl], q2[p][d][:, tcl], True, True,
                               ['k2%d%d' % (p, d), 'q2%d%d' % (p, d)], [BK[5]])
                        tt('dve', A[:], PB[5][:, :], mask[:], ALU.mult, [BK[5], 'mask'], [ak])
                        mm(PB[3][:, :], khf[p][:, t, :], vblk[p][:, t, :, :].rearrange("p j v -> p (j v)"), True, True,
                           ['khf%d' % p, 'vblk%d' % p], [BK[3]])
                        for i4 in range(4):
                            mm(PB[6][:, 0:128], vall[:, tile, :], A[:, i4 * 128:(i4 + 1) * 128], i4 == 0, False,
                               ['vall', ak], [BK[6]])
                        for j in range(4):
                            ch = tile * 4 + j
                            c = cur[0]
                            mm(PB[6][:, 32 * j:32 * j + 32], Sb[0][c][:], qfT[p][:, t * 128 + 32 * j:t * 128 + 32 * j + 32],
                               False, j == 3, ['Sb0%d' % c, 'qfT%d' % p], [BK[6]])
                            step_state(0, ch, PB[3][:, j * 128:(j + 1) * 128], BK[3], ch % 8 == 7)
                        cp('act', oacc[:, tcg], PB[6][:, 0:128], [BK[6]], ['oacc'])
                        yield

                stage_proj(0)
                for _ in stage_elem(0):
                    pass
                for blk in range(NB):
                    gs = stage_scan(blk)
                    if blk + 1 < NB:
                        ge = stage_elem(blk + 1)
                    else:
                        ge = iter(())
                    done_s = done_e = False
                    while not (done_s and done_e):
                        if not done_s:
                            try:
                                next(gs)
                            except StopIteration:
                                done_s = True
                        if not done_e:
                            try:
                                next(ge)
                            except StopIteration:
                                done_e = True
                if STOP == "B3":
                    raise StopIteration
                szs = [(q1[pp][dd], 'q1%d%d' % (pp, dd)) for pp in range(2) for dd in range(2)] + \
                      [(k1[pp][dd], 'k1%d%d' % (pp, dd)) for pp in range(2) for dd in range(2)]
                for blk in range(NB):
                    cols = slice(blk * 512, (blk + 1) * 512)
                    bk = blk % 3
                    for kc in range(8):
                        mm(PB[bk][:, :], W[:, kc, 4, :], hT[:, kc, cols], kc == 0, kc == 7, [wk, 'hT'], [BK[bk]])
                    act(qs[:], PB[bk][:, :], AF.Sigmoid, [BK[bk]], ['qs'])
                    tt('dve', szs[blk][0][:], PB[bk][:, :], qs[:], ALU.mult, [BK[bk], 'qs'], [szs[blk][1]])

                def fin1(tile):
                    q_ = tile % 2
                    tcg = slice(tile * 128, (tile + 1) * 128)
                    tt('dve', ot2[q_][:], oacc[:, tcg], PB[6][:, q_ * 128:(q_ + 1) * 128], ALU.add, ['oacc', BK[6]], ['ot%d' % q_])
                    act(osq2[q_][:], ot2[q_][:], AF.Square, ['ot%d' % q_], ['osq%d' % q_])
                    mm(PB[5][:, q_ * 128:(q_ + 1) * 128], onesb[:], osq2[q_][:], True, True, ['onesb', 'osq%d' % q_], [BK[5]])
                    act(lnm2[q_][:], PB[5][:, q_ * 128:(q_ + 1) * 128], AF.Ln, [BK[5], 'epsc'], ['lnm%d' % q_],
                        bias=epsc[:, 0:1], scale=1.0 / 128)
                    act(rs22[q_][:], lnm2[q_][:], AF.Exp, ['lnm%d' % q_], ['rs2%d' % q_], scale=-0.5)

                def fin2(tile):
                    q_ = tile % 2
                    blk, t = divmod(tile, 4)
                    tcg = slice(tile * 128, (tile + 1) * 128)
                    tt('dve', t12[q_][:], ot2[q_][:], rs22[q_][:], ALU.mult, ['ot%d' % q_, 'rs2%d' % q_], ['t1%d' % q_])
                    Y, yk = yst[q_], 'yst%d' % q_
                    stt(Y[:], t12[q_][:], gnw[:, l * 8 + h:l * 8 + h + 1], szs[blk][0][:, t * 128:(t + 1) * 128],
                        ALU.mult, ALU.mult, ['t1%d' % q_, 'gnw', szs[blk][1]], [yk])
                    dma('sp', ya_d[h][:, tcg], Y[:], [yk], ['ya_%d_%d' % (h, blk)])

                prev_tile = None
                for tile in reversed(range(NT)):
                    blk, t = divmod(tile, 4)
                    q_ = tile % 2
                    if t == 3:
                        vb, vbk = vblk[blk % 2], 'vblk%d' % (blk % 2)
                        for j in range(4):
                            dma('sp', vb[32 * j:32 * j + 32, :, j, :], vall[32 * j:32 * j + 32, blk * 4:(blk + 1) * 4, :],
                                ['vall'], [vbk])
                    mm(PB[3][:, :], khb[:, tile, :], vb[:, t, :, :].rearrange("p j v -> p (j v)"), True, True,
                       ['khb', vbk], [BK[3]])
                    first = True
                    for j in (3, 2, 1, 0):
                        ch = tile * 4 + j
                        c = cur[1]
                        mm(PB[6][:, q_ * 128 + 32 * j:q_ * 128 + 32 * j + 32], Sb[1][c][:],
                           qbT[:, tile * 128 + 32 * j:tile * 128 + 32 * j + 32],
                           first, j == 0, ['Sb1%d' % c, 'qbT'], [BK[6]])
                        first = False
                        step_state(1, ch, PB[3][:, j * 128:(j + 1) * 128], BK[3], ch % 8 == 0)
                    if prev_tile is not None:
                        fin2(prev_tile)
                    fin1(tile)
                    prev_tile = tile
                fin2(prev_tile)
            if P.pending is not None:
                P.end_region()

        P.barrier()
        if STOP == "B":
            raise StopIteration
        with ExitStack() as ps:
            ucs = sb("ucs", [128, NT, 4, 256], BF16, st=ps)
            szB = sb("szB", [128, 4, NTOK], BF16, st=ps)
            wu = [sb("wu%d" % i, [128, 8, 128], BF16, st=ps) for i in range(2)]
            wz = [sb("wz%d" % i, [128, 8, 128], BF16, st=ps) for i in range(2)]
            uT2 = [sb("uT%d" % i, [128, 512], BF16, st=ps) for i in range(2)]
            NTB = 6
            tb = [sb("tb%d" % i, [128, 1024], BF16, st=ps) for i in range(NTB)]
            ybs = [sb("ybs%d" % i, [128, 512], BF16, st=ps) for i in range(2)]
            if os.environ.get("KS_C", SCHED_DEFAULT["KS_C"]) == "1":
                P.begin_region()
            for g in range(4):
                dma('pool', wu[g % 2][:], win_d[l][:, 5120 + g * 128:5120 + (g + 1) * 128].rearrange(
                    "(kc p) c -> p kc c", p=128), [], ['wu%d' % (g % 2)])
                dma('pool', wz[g % 2][:], win_d[l][:, 5632 + g * 128:5632 + (g + 1) * 128].rearrange(
                    "(kc p) c -> p kc c", p=128), [], ['wz%d' % (g % 2)])
                for blk in range(NB):
                    cols = slice(blk * 512, (blk + 1) * 512)
                    for kc in range(8):
                        mm(PB[4][:, :], wu[g % 2][:, kc, :], hT[:, kc, cols], kc == 0, kc == 7,
                           ['wu%d' % (g % 2), 'hT'], [BK[4]])
                    uT, utk = uT2[blk % 2], 'uT%d' % (blk % 2)
                    cp('act', uT[:], PB[4][:, :], [BK[4]], [utk])
                    for tp in range(2):
                        for t2 in range(2):
                            t = tp * 2 + t2
                            mm(PB[5][:, t2 * 256:(t2 + 1) * 256], uT[:, t * 128:(t + 1) * 128], cs[:], True, True,
                               [utk, 'cs'], [BK[5]])
                        cp('dve', ucs[:, blk * 4 + tp * 2:blk * 4 + tp * 2 + 2, g, :],
                           PB[5][:, :].rearrange("p (t c) -> p t c", t=2), [BK[5]], ['ucs'])
                    for kc in range(8):
                        mm(PB[6][:, :], wz[g % 2][:, kc, :], hT[:, kc, cols], kc == 0, kc == 7,
                           ['wz%d' % (g % 2), 'hT'], [BK[6]])
                    act(szB[:, g, cols], PB[6][:, :], AF.Silu, [BK[6]], ['szB'])
            n_t = 0
            for kb in range(NB):
                kcols = slice(kb * 512, (kb + 1) * 512)
                for a in range(NT):
                    T_, tk = tb[n_t % NTB], 'tb%d' % (n_t % NTB)
                    n_t += 1
                    dma('sp', T_[:], tbl_d[kb, a], [], [tk])
                    for g in range(4):
                        mm(PB[g][:, :], ucs[:, a, g, 0:128], T_[:, 0:512], a == 0, False, ['ucs', tk], [BK[g]])
                        mm(PB[g][:, :], ucs[:, a, g, 128:256], T_[:, 512:1024], False, a == NT - 1, ['ucs', tk], [BK[g]])
                for g in range(4):
                    Yb, ybk = ybs[g % 2], 'ybs%d' % (g % 2)
                    tt('dve', Yb[:], PB[g][:, :], szB[:, g, kcols], ALU.mult, [BK[g], 'szB'], [ybk])
                    dma('sp', yb_d[g][:, kcols], Yb[:], [ybk], ['yb_%d_%d' % (g, kb)])
            if P.pending is not None:
                P.end_region()

        P.barrier()
        if STOP == "C":
            raise StopIteration
        with ExitStack() as ps:
            wpa = sb("wpa", [128, 8, D], BF16, st=ps)
            wpb = sb("wpb", [128, 4, D], BF16, st=ps)
            wo = sb("wo", [128, 8, D], BF16, st=ps)
            wg = sb("wg", [128, 8, 2 * D], BF16, st=ps)
            ya = sb("ya", [128, 8, 512], BF16, st=ps)
            yb = sb("yb", [128, 4, 512], BF16, st=ps)
            mg = sb("mg", [128, 8, 512], BF16, st=ps)
            sA = sb("sA", [128, 512], st=ps)
            sB_ = sb("sB", [128, 512], st=ps)
            m1 = sb("m1", [128, 512], st=ps)
            m2 = sb("m2", [128, 512], st=ps)
            xt = [sb("dxt%d" % i, [128, D], st=ps) for i in range(2)]
            xw = [sb("dxw%d" % i, [128, D], st=ps) for i in range(2)]
            tmp = sb("dtmp", [128, D], st=ps)
            fnw = sb("fnw", [128, D], st=ps)
            junk = sb("djunk", [128, D], st=ps)
            ss = sb("dss", [128, 1], st=ps)
            ms = sb("dms", [128, 1], st=ps)
            rstd = sb("drstd", [128, 1], st=ps)
            if os.environ.get("KS_D", SCHED_DEFAULT["KS_D"]) == "1":
                P.begin_region()
            for kc in range(8):
                rws = slice(kc * 128, (kc + 1) * 128)
                cdma(wpa[:, kc, :], wpa_d[l][rws, :], [], ['wpa'])
                cdma(wo[:, kc, :], wo_d[l][rws, :], [], ['wo'])
                cdma(wg[:, kc, :], win_d[l][rws, 6144:8192], [], ['wg'])
                if kc < 4:
                    cdma(wpb[:, kc, :], wpb_d[l][rws, :], [], ['wpb'])
            if last:
                dma('sp', fnw[:], fnw_d, [], ['fnw'])
            for blk in range(NB):
                cols = slice(blk * 512, (blk + 1) * 512)
                dma('sp', ya[:], ya_d[:, :, cols].rearrange("h p n -> p h n"),
                    ['ya_%d_%d' % (h, blk) for h in range(H)], ['ya'])
                dma('sp', yb[:], yb_d[:, :, cols].rearrange("g p n -> p g n"),
                    ['yb_%d_%d' % (g, blk) for g in range(4)], ['yb'])
                if blk == 0 and l == 0:
                    for hh in range(8):
                        dbg(24 + hh, ya[:, hh, :], 'ya', bf=True)
                    for g in range(4):
                        dbg(32 + g, yb[:, g, :], 'yb', bf=True)
                for dc in range(8):
                    dcs = slice(dc * 128, (dc + 1) * 128)
                    for hh in range(8):
                        mm(PB[0][:, :], wpa[:, hh, dcs], ya[:, hh, :], hh == 0, hh == 7, ['wpa', 'ya'], [BK[0]])
                    for g in range(4):
                        mm(PB[1][:, :], wpb[:, g, dcs], yb[:, g, :], g == 0, g == 3, ['wpb', 'yb'], [BK[1]])
                    for kc in range(8):
                        mm(PB[2][:, :], wg[:, kc, dcs], hT[:, kc, cols], kc == 0, kc == 7, ['wg', 'hT'], [BK[2]])
                    for kc in range(8):
                        mm(PB[3][:, :], wg[:, kc, D + dc * 128:D + (dc + 1) * 128], hT[:, kc, cols], kc == 0, kc == 7,
                           ['wg', 'hT'], [BK[3]])
                    act(sA[:], PB[2][:, :], AF.Sigmoid, [BK[2]], ['sA'])
                    act(sB_[:], PB[3][:, :], AF.Sigmoid, [BK[3]], ['sB'])
                    tt('dve', m1[:], PB[0][:, :], sA[:], ALU.mult, [BK[0], 'sA'], ['m1'])
                    tt('dve', m2[:], PB[1][:, :], sB_[:], ALU.mult, [BK[1], 'sB'], ['m2'])
                    tt('pool', mg[:, dc, :], m1[:], m2[:], ALU.add, ['m1', 'm2'], ['mg'])
                for t in range(4):
                    tile = blk * 4 + t
                    rows = slice(tile * 128, (tile + 1) * 128)
                    X, xk = xt[tile % 2], 'dxt%d' % (tile % 2)
                    XW, xwk = xw[tile % 2], 'dxw%d' % (tile % 2)
                    dma('sp', X[:], xcur_d[rows, :], ['xcur%d' % tile], [xk])
                    for half in range(2):
                        hc = slice(half * 512, (half + 1) * 512)
                        bk = 4 + half
                        for dc in range(8):
                            mm(PB[bk][:, :], mg[:, dc, t * 128:(t + 1) * 128], wo[:, dc, hc], dc == 0, dc == 7,
                               ['mg', 'wo'], [BK[bk]])
                        tt('dve', tmp[:, hc], PB[bk][:, :], gtB[:, hc], ALU.mult, [BK[bk], 'gtB'], ['dtmp'])
                        tt('pool', XW[:, hc], tmp[:, hc], X[:, hc], ALU.add, ['dtmp', xk], [xwk])
                    if not last:
                        dma('sp', xcur_d[rows, :], XW[:], [xwk], ['xcur%d' % tile])
                    else:
                        act(junk[:], XW[:], AF.Square, [xwk], ['djunk'])
                        rsum(ss[:], junk[:], ['djunk'], ['dss'])
                        ts('dve', ms[:], ss[:], 1.0 / D, EPS, ALU.mult, ALU.add, ['dss'], ['dms'])
                        act(ms[:], ms[:], AF.Sqrt, ['dms'], ['dms'])
                        recip(rstd[:], ms[:], ['dms'], ['drstd'])
                        stt(tmp[:], XW[:], rstd[:, 0:1], fnw[:], ALU.mult, ALU.mult, [xwk, 'drstd', 'fnw'], ['dtmp'])
                        dma('sp', y_d[rows, :], tmp[:], ['dtmp'], [])
            if P.pending is not None:
                P.end_region()
        P.barrier()
    es.close()


_CACHE = {}


def get_nc(NT, DEPTH, NSO):
    key = (NT, DEPTH, NSO)
    if key not in _CACHE:
        nc1 = bass.Bass("TRN2", target_bir_lowering=False)
        P1 = Prog(nc1, None, None)
        build(nc1, P1, NT, DEPTH, NSO)
        nc2 = bass.Bass("TRN2", target_bir_lowering=False)
        es = ExitStack()
        P2 = Prog(nc2, es, P1.need)
        build(nc2, P2, NT, DEPTH, NSO)
        es.close()
        _CACHE[key] = nc2
    return _CACHE[key]


def consts(NT):
    bf = ml_dtypes.bfloat16
    c = {}
    c["ident"] = np.eye(128, dtype=np.float32).astype(bf)
    s = np.arange(128)[:, None]
    t = np.arange(128)[None, :]
    same = (s // 32) == (t // 32)
    hs, ht = (s % 32) // 16, (t % 32) // 16
    c["mask"] = np.concatenate([same & (hs == ht) & (s <= t), same & (hs == 0) & (ht == 1),
                                same & (hs == ht) & (s >= t), same & (hs == 1) & (ht == 0)], axis=1).astype(np.float32)
    sm = np.ones((128, 512), np.float32)
    sm[:, ::16] = 0.0
    c["smask"] = sm
    j = np.arange(128)
    ang = 2 * np.pi * np.outer(j, j) / 128.0
    c["cs"] = (np.concatenate([np.cos(ang), -np.sin(ang)], axis=1) / np.sqrt(128.0)).astype(np.float32).astype(bf)
    return c


def dft_table(NT, L):
    NTOK = NT * 128
    NB = NT // 4
    n = np.arange(NTOK)
    tblc = np.zeros((NTOK, NTOK), np.float32)
    tbls = np.zeros((NTOK, NTOK), np.float32)
    for s0 in range(0, NTOK, L):
        m = np.arange(L)
        ang = 2 * np.pi * (np.outer(m, m) % L) / L
        tblc[s0:s0 + L, s0:s0 + L] = np.cos(ang) / np.sqrt(L)
        tbls[s0:s0 + L, s0:s0 + L] = np.sin(ang) / np.sqrt(L)
    out = np.zeros((NB, NT, 128, 1024), ml_dtypes.bfloat16)
    for kb in range(NB):
        cc = tblc[:, kb * 512:(kb + 1) * 512].reshape(NT, 128, 512)
        sn = tbls[:, kb * 512:(kb + 1) * 512].reshape(NT, 128, 512)
        out[kb, :, :, 0:512] = cc.astype(ml_dtypes.bfloat16)
        out[kb, :, :, 512:1024] = sn.astype(ml_dtypes.bfloat16)
    return out


def pos_table(L):
    GRID_W = 64
    rows = L // GRID_W
    r = np.repeat(np.arange(rows, dtype=np.float32), GRID_W)
    col = np.tile(np.arange(GRID_W, dtype=np.float32), rows)
    nf = D // 4
    freqs = (1.0 / (np.float32(10000.0) ** (np.arange(nf, dtype=np.float32) / np.float32(nf)))).astype(np.float32)

    def emb(p):
        a = (p[:, None] * freqs[None, :]).astype(np.float32)
        return np.concatenate([np.sin(a), np.cos(a)], axis=-1)
    return np.concatenate([emb(r), emb(col)], axis=-1).astype(np.float32)


def shared_inputs(DEPTH, norm_w, w_ada, b_ada, w_in, lb_raw, gnorm_w, w_pa, w_pb, w_o, final_norm_w):
    f = np.float32
    m = {}
    m["nw"] = np.ascontiguousarray(np.asarray(norm_w, f).reshape(DEPTH, 8, 128).transpose(2, 0, 1).reshape(128, DEPTH * 8))
    m["w_ada"] = np.ascontiguousarray(np.asarray(w_ada, f))
    ba = np.asarray(b_ada, f)
    m["bada_p"] = np.ascontiguousarray(ba[:, :2048].reshape(DEPTH, 16, 128).transpose(2, 0, 1).reshape(128, DEPTH * 16))
    m["bada_g"] = np.ascontiguousarray(np.broadcast_to(ba[:, None, 2048:], (DEPTH, 128, D)))
    m["w_in"] = np.ascontiguousarray(np.asarray(w_in, f))
    m["lbr"] = np.ascontiguousarray(np.asarray(lb_raw, f).reshape(DEPTH, 2, 8, 128).transpose(3, 0, 1, 2).reshape(128, DEPTH * 16))
    m["gnw"] = np.ascontiguousarray(np.asarray(gnorm_w, f).reshape(DEPTH, 8, 128).transpose(2, 0, 1).reshape(128, DEPTH * 8))
    m["w_pa"] = np.ascontiguousarray(np.asarray(w_pa, f))
    m["w_pb"] = np.ascontiguousarray(np.asarray(w_pb, f))
    m["w_o"] = np.ascontiguousarray(np.asarray(w_o, f))
    m["fnw"] = np.ascontiguousarray(np.broadcast_to(np.asarray(final_norm_w, f)[None, :], (128, D)))
    return m


def core_inputs(shared, cst, x, pos, cvec, s0, keepv, tbl):
    m = dict(shared)
    m.update(cst)
    m["x"] = np.ascontiguousarray(x, dtype=np.float32)
    m["pos"] = np.ascontiguousarray(pos, dtype=np.float32)
    m["cv"] = np.ascontiguousarray(np.asarray(cvec, np.float32).reshape(8, 128).T)
    m["s0"] = np.ascontiguousarray(np.asarray(s0, np.float32).reshape(-1, 128, 128))
    m["keep"] = np.full((128, 1), keepv, np.float32)
    m["tbl"] = tbl
    return m


def kernel(x_prompt, x_sample, state_hgrn, c, c_ctx, norm_w, w_ada, b_ada, w_in, lb_raw,
           gnorm_w, w_pa, w_pb, w_o, final_norm_w):
    DEPTH, NT, NSO = 4, 32, 8
    NTOK = NT * 128
    x_prompt = np.asarray(x_prompt, np.float32)
    x_sample = np.asarray(x_sample, np.float32)
    state_hgrn = np.asarray(state_hgrn, np.float32)
    c = np.asarray(c, np.float32)
    c_ctx = np.asarray(c_ctx, np.float32)
    shared = shared_inputs(DEPTH, norm_w, w_ada, b_ada, w_in, lb_raw, gnorm_w, w_pa, w_pb, w_o, final_norm_w)
    cst = consts(NT)
    tbl_s = dft_table(NT, NTOK)
    tbl_p = dft_table(NT, SEG)
    pos = pos_table(NTOK)
    zpos = np.zeros((NTOK, D), np.float32)
    zs0 = np.zeros((DEPTH * 2 * H, 128, 128), np.float32)
    in_maps = []
    for b in range(4):
        in_maps.append(core_inputs(shared, cst, x_sample[b], pos, c[b], state_hgrn[b], 1.0, tbl_s))
    for k in range(4):
        xp = np.zeros((NTOK, D), np.float32)
        xp[:8 * SEG] = x_prompt[8 * k:8 * k + 8].reshape(8 * SEG, D)
        in_maps.append(core_inputs(shared, cst, xp, zpos, c_ctx, zs0, 0.0, tbl_p))
    nc = get_nc(NT, DEPTH, NSO)
    res = run_bass_kernel_spmd(nc, in_maps, core_ids=list(range(8)))
    R = res.results
    y_sample = np.stack([np.asarray(R[b]["y"], np.float32) for b in range(4)], axis=0)
    y_prompt = np.concatenate([np.asarray(R[4 + k]["y"], np.float32)[:8 * SEG].reshape(8, SEG, D) for k in range(4)], axis=0)
    new_state = np.concatenate([np.asarray(R[4 + k]["st"], np.float32).reshape(NSO, DEPTH, 2, H, 128, 128)
                                for k in range(4)], axis=0)
    return (y_prompt, y_sample, new_state)
```

```python
import os
import numpy as np
import ml_dtypes
from contextlib import ExitStack
import concourse.bass as bass
import concourse.mybir as mybir
from concourse.bass_utils import run_bass_kernel_spmd

F32 = mybir.dt.float32
BF16 = mybir.dt.bfloat16
AF = mybir.ActivationFunctionType
ALU = mybir.AluOpType
D = 1024
H = 8
EPS = 1e-6
SEG = 256
NSP, NPL = 16, 8


class Prog:
    LAT = 0.15

    def __init__(self, nc, es, plan):
        self.nc, self.plan, self.n = nc, plan, 0
        self.lastw, self.rd = {}, {}
        self.need = set()
        self.ev = {}
        self.isdma, self.engof = [], []
        self.waited = {}
        self.cnt = {'pe': 0, 'act': 0, 'dve': 0, 'pool': 0, 'sp': 0}
        self.E = dict(pe=nc.tensor, act=nc.scalar, dve=nc.vector, pool=nc.gpsimd, sp=nc.sync)
        self.sems = []
        self.semidx = {}
        self.dma_n = {'sp': 0, 'pool': 0}
        self.slot_last = {}
        self.last_real = {}
        self.pending = None
        if plan is not None:
            for e in ('pe', 'act', 'dve', 'pool', 'sp'):
                self.semidx[e] = len(self.sems)
                self.sems.append(es.enter_context(nc.semaphore('s_' + e)))
            for q, n in (('sp', NSP), ('pool', NPL)):
                for i in range(n):
                    self.semidx[(q, i)] = len(self.sems)
                    self.sems.append(es.enter_context(nc.semaphore('d_%s%d' % (q, i))))

    def barrier(self):
        assert self.pending is None
        extra = [i for (i, v) in self.slot_last.values()] + list(self.last_real.values())
        for e in ('pe', 'act', 'dve', 'pool', 'sp'):
            self.op(e, lambda e=e: self.E[e].nop(), [], [], extra=extra, is_nop=True)

    def begin_region(self):
        self.pending = []

    def end_region(self):
        recs, self.pending = self.pending, None
        if self.plan is None or not recs:
            for rec in recs:
                self._emit(rec)
            return
        LAT = float(os.environ.get("KLAT", "0.25"))
        TBL = float(os.environ.get("KTBL", "1.3"))
        SLACK = float(os.environ.get("KSLACK", "0.6"))
        inreg = {rec['i']: rec for rec in recs}
        nleft = {}
        users = {}
        for rec in recs:
            dd = [d for d in rec['deps'] if d in inreg]
            nleft[rec['i']] = len(dd)
            for d in dd:
                users.setdefault(d, []).append(rec['i'])
        blevel = {}
        for rec in reversed(recs):
            i = rec['i']
            c = 2.0 if rec['dma'] else rec['cost']
            m = 0.0
            for u in users.get(i, ()):
                if blevel[u] > m:
                    m = blevel[u]
            blevel[i] = c + m + LAT
        ready = {e: set() for e in self.E}
        for rec in recs:
            if nleft[rec['i']] == 0:
                ready[rec['eng']].add(rec['i'])
        t_eng = {e: 0.0 for e in self.E}
        fin = {}
        fam = [None]
        n_done = 0
        engof = self.engof

        def start_of(i, e):
            st = t_eng[e]
            for d in inreg[i]['deps']:
                f = fin.get(d)
                if f is not None:
                    f += LAT if engof[d] != e else 0.05
                    if f > st:
                        st = f
            if e == 'act':
                fm = inreg[i]['fam']
                if fm is not None and fam[0] is not None and fm != fam[0]:
                    st += TBL
            return st

        while n_done < len(recs):
            best = None
            for e in self.E:
                if not ready[e]:
                    continue
                cands = [(start_of(i, e), i) for i in ready[e]]
                smin = min(c[0] for c in cands)
                pick = None
                for st, i in cands:
                    if st <= smin + SLACK:
                        key = (-blevel[i], i)
                        if pick is None or key < pick[0]:
                            pick = (key, st, i)
                _, st, i = pick
                if best is None or (st, -blevel[i], i) < (best[0], -blevel[best[1]], best[1]):
                    best = (st, i, e)
            st, i, e = best
            ready[e].discard(i)
            rec = inreg[i]
            if e == 'act' and rec['fam'] is not None:
                fam[0] = rec['fam']
            if rec['dma']:
                fin[i] = st + 2.0
                t_eng[e] = st + 0.08
            else:
                fin[i] = st + rec['cost']
                t_eng[e] = st + rec['cost']
            self._emit(rec)
            n_done += 1
            for u in users.get(i, ()):
                nleft[u] -= 1
                if nleft[u] == 0:
                    ready[inreg[u]['eng']].add(u)

    def op(self, eng, fn, r=(), w=(), dma=False, extra=(), is_nop=False, cost=0.3, fam=None):
        i = self.n
        self.n += 1
        deps = set(extra)
        for k in r:
            if k in self.lastw:
                deps.add(self.lastw[k])
        for k in w:
            if k in self.lastw:
                deps.add(self.lastw[k])
            rr = self.rd.get(k)
            if rr:
                deps.update(rr.values())
        self.isdma.append(dma)
        self.engof.append(eng)
        region = self.pending is not None
        for k in r:
            key = ('dma', i) if (dma or region) else eng
            self.rd.setdefault(k, {})[key] = i
        for k in w:
            self.lastw[k] = i
            self.rd[k] = {}
        rec = dict(i=i, eng=eng, fn=fn, deps=deps, dma=dma, is_nop=is_nop, cost=cost, fam=fam)
        real = [d for d in deps if self.isdma[d] or dma or self.engof[d] != eng or eng != 'pe']
        if self.plan is None:
            self.need.update(real)
        if region:
            self.pending.append(rec)
        else:
            self._emit(rec)
        return i

    def _emit(self, rec):
        i, eng, dma = rec['i'], rec['eng'], rec['dma']
        if not dma and not rec['is_nop']:
            self.last_real[eng] = i
        deps = set(rec['deps'])
        slot = prev = None
        if dma:
            slot = self.dma_n[eng] % (NSP if eng == 'sp' else NPL)
            self.dma_n[eng] += 1
            prev = self.slot_last.get((eng, slot))
            if prev is not None:
                deps.add(prev[0])
        if self.plan is None:
            if dma:
                self.slot_last[(eng, slot)] = (i, 0)
            return
        real = [d for d in deps if self.isdma[d] or dma or self.engof[d] != eng or eng != 'pe']
        E = self.E[eng]
        for d in real:
            s, v = self.ev[d]
            if self.waited.get((eng, s), 0) < v:
                E.wait_ge(self.sems[s], v)
                self.waited[(eng, s)] = v
        inst = rec['fn']()
        if dma:
            s = self.semidx[(eng, slot)]
            v = (prev[1] if prev else 0) + 16
            inst.then_inc(self.sems[s], 16)
            self.ev[i] = (s, v)
            self.slot_last[(eng, slot)] = (i, v)
        elif i in self.plan:
            self.cnt[eng] += 1
            s = self.semidx[eng]
            inst.then_inc(self.sems[s], 1)
            self.ev[i] = (s, self.cnt[eng])

    def finish(self):
        if self.plan is None:
            return
        assert self.pending is None
        for (q, slot), (i, v) in self.slot_last.items():
            self.nc.sync.wait_ge(self.sems[self.semidx[(q, slot)]], v)


import os
STOP = os.environ.get("KSTOP", "")
SCHED_DEFAULT = {"KS_ADA": "1", "KS_A": "1", "KS_C": "1", "KS_D": "1"}


def build(nc, P, NT, DEPTH, NSO):
    try:
        _build(nc, P, NT, DEPTH, NSO)
    except StopIteration:
        if P.pending is not None:
            P.end_region()
    P.finish()


def _build(nc, P, NT, DEPTH, NSO):
    NTOK = NT * 128
    NB = NT // 4
    NCH = NT * 4
    es = ExitStack()

    def din(name, shape, dt=F32):
        return nc.dram_tensor(name, list(shape), dt, kind="ExternalInput").ap()

    x_d = din("x", [NTOK, D])
    pos_d = din("pos", [NTOK, D])
    cv_d = din("cv", [128, 8])
    s0_d = din("s0", [DEPTH * 2 * H, 128, 128])
    keep_d = din("keep", [128, 1])
    nw_d = din("nw", [128, DEPTH * 8])
    wada_d = din("w_ada", [DEPTH, D, 3 * D])
    badap_d = din("bada_p", [128, DEPTH * 16])
    badag_d = din("bada_g", [DEPTH, 128, D])
    win_d = din("w_in", [DEPTH, D, 8192])
    lbr_d = din("lbr", [128, DEPTH * 16])
    gnw_d = din("gnw", [128, DEPTH * 8])
    wpa_d = din("w_pa", [DEPTH, D, D])
    wpb_d = din("w_pb", [DEPTH, 512, D])
    wo_d = din("w_o", [DEPTH, D, D])
    fnw_d = din("fnw", [128, D])
    ident_d = din("ident", [128, 128], BF16)
    mask_d = din("mask", [128, 512])
    smask_d = din("smask", [128, 512])
    cs_d = din("cs", [128, 256], BF16)
    tbl_d = din("tbl", [NB, NT, 128, 1024], BF16)
    y_d = nc.dram_tensor("y", [NTOK, D], F32, kind="ExternalOutput").ap()
    st_d = nc.dram_tensor("st", [NSO * DEPTH * 2 * H, 128, 128], F32, kind="ExternalOutput").ap()
    xcur_d = nc.dram_tensor("xcur", [NTOK, D], F32).ap()
    ya_d = nc.dram_tensor("ya_s", [H, 128, NTOK], BF16).ap()
    yb_d = nc.dram_tensor("yb_s", [4, 128, NTOK], BF16).ap()

    DBG = os.environ.get("KDBG", "") == "1"
    if DBG:
        dbg_d = nc.dram_tensor("dbg", [40, 128, 512], F32, kind="ExternalOutput").ap()

    dscr_n = [0]

    def dbg(slot, ap, key, ncols=512, bf=False):
        if DBG and ncols < 2:
            t_ = dscr_t[dscr_n[0]]
            dscr_n[0] += 1
            k_ = 'dscr%d' % dscr_n[0]
            P.op('pool', lambda: nc.gpsimd.memset(t_[:], 0.0), [], [k_])
            P.op('pool', lambda: nc.gpsimd.tensor_copy(out=t_[:, 0:1], in_=ap), [key, k_], [k_])
            ap, key, ncols = t_[:], k_, 2
        if DBG:
            q = 'pool' if bf else 'sp'
            eng = nc.gpsimd if bf else nc.sync
            P.op(q, lambda: eng.dma_start(out=dbg_d[slot][:, 0:ncols], in_=ap), [key], [], dma=True)

    uid = [0]

    def sb(name, shape, dt=F32, st=es):
        uid[0] += 1
        return st.enter_context(nc.sbuf_tensor("s%d_%s" % (uid[0], name), list(shape), dt))

    PB = [es.enter_context(nc.psum_tensor("pb%d" % i, [128, 512], F32)) for i in range(7)]
    PB.append(es.enter_context(nc.psum_tensor("pb7", [128, 1024], BF16)))
    BK = ["B%d" % i for i in range(8)]

    dscr_t = [sb("dscr%d" % i, [128, 2]) for i in range(8)] if DBG else []
    hT = sb("hT", [128, 8, NTOK], BF16)
    ident = sb("ident", [128, 128], BF16)
    mask = sb("mask", [128, 512])
    smask = sb("smask", [128, 512])
    cs = sb("cs", [128, 256], BF16)
    onesf = sb("onesf", [128, 128])
    epsc = sb("epsc", [128, 1])
    keep = sb("keep", [128, 1])
    cv = sb("cv", [128, 8])
    cvs = sb("cvs", [128, 8])
    scB = sb("scB", [128, 8, 128], BF16)
    cvb = sb("cvb", [128, 8], BF16)
    onesb = sb("onesb", [128, 128], BF16)
    nw = sb("nw", [128, DEPTH * 8])
    badap = sb("badap", [128, DEPTH * 16])
    lbr = sb("lbr", [128, DEPTH * 16])
    lbe = sb("lbe", [128, DEPTH * 16])
    lbt = sb("lbt", [128, DEPTH * 16])
    lbs = sb("lbs", [128, 16])
    gnw = sb("gnw", [128, DEPTH * 8])
    shsc = sb("shsc", [128, 16])
    sc1p = sb("sc1p", [128, 8])
    gtB = sb("gtB", [128, D])

    def fsz(ap):
        n = 1
        for x in ap.shape[1:]:
            n *= int(x)
        return n

    def dma(q, out, in_, r, w):
        eng = nc.sync if q == 'sp' else nc.gpsimd
        P.op(q, lambda: eng.dma_start(out=out, in_=in_), r, w, dma=True)

    def cdma(out, in_, r, w, b=512):
        n = out.shape[-1]
        if len(out.shape) == 2 and n > b:
            out = out.rearrange("p (a b) -> p a b", b=b)
            in_ = in_.rearrange("p (a b) -> p a b", b=b)
        dma('pool', out, in_, r, w)

    def mm(out, lhsT, rhs, start, stop, r, w, **kw):
        P.op('pe', lambda: nc.tensor.matmul(out, lhsT=lhsT, rhs=rhs, start=start, stop=stop, **kw), r, w,
             cost=0.04 + fsz(rhs) / 2000.0)

    def tr(out, in_, r, w):
        P.op('pe', lambda: nc.tensor.transpose(out=out, in_=in_, identity=ident[:]), r, w, cost=0.1)

    FAM = {AF.Sigmoid: 'sig', AF.Silu: 'silu', AF.Ln: 'ln', AF.Exp: 'ln', AF.Sqrt: 'sqrt'}

    def act(out, in_, func, r, w, **kw):
        P.op('act', lambda: nc.scalar.activation(out=out, in_=in_, func=func, **kw), r, w,
             cost=0.2 + fsz(out) / 1400.0, fam=FAM.get(func))

    def tt(e, out, in0, in1, op, r, w):
        eng = nc.vector if e == 'dve' else nc.gpsimd
        P.op(e, lambda: eng.tensor_tensor(out=out, in0=in0, in1=in1, op=op), r, w,
             cost=(0.08 + fsz(out) / 960.0) if e == 'dve' else (0.5 + fsz(out) / 400.0))

    def ts(e, out, in0, s1, s2, op0, op1, r, w):
        eng = nc.vector if e == 'dve' else nc.gpsimd
        c_ = (0.08 + fsz(out) / 960.0) if e == 'dve' else (0.5 + fsz(out) / 400.0)
        if op1 is None:
            P.op(e, lambda: eng.tensor_scalar(out=out, in0=in0, scalar1=s1, scalar2=None, op0=op0), r, w, cost=c_)
        else:
            P.op(e, lambda: eng.tensor_scalar(out=out, in0=in0, scalar1=s1, scalar2=s2, op0=op0, op1=op1), r, w, cost=c_)

    def stt(out, in0, scalar, in1, op0, op1, r, w):
        P.op('dve', lambda: nc.vector.scalar_tensor_tensor(out=out, in0=in0, scalar=scalar, in1=in1,
                                                           op0=op0, op1=op1), r, w, cost=0.08 + fsz(out) / 960.0 + 0.1)

    def scan(out, d0, d1, r, w):
        P.op('dve', lambda: nc.vector.tensor_tensor_scan(out=out, data0=d0, data1=d1, initial=0.0,
                                                         op0=ALU.mult, op1=ALU.add), r, w, cost=0.08 + fsz(out) / 480.0)

    def rsum(out, in_, r, w):
        P.op('dve', lambda: nc.vector.reduce_sum(out=out, in_=in_, axis=mybir.AxisListType.X), r, w,
             cost=0.08 + fsz(in_) / 960.0)

    def recip(out, in_, r, w):
        P.op('dve', lambda: nc.vector.reciprocal(out=out, in_=in_), r, w, cost=0.08 + fsz(out) / 120.0)

    def cp(e, out, in_, r, w):
        if e == 'act':
            P.op('act', lambda: nc.scalar.activation(out=out, in_=in_, func=AF.Identity), r, w,
                 cost=0.2 + fsz(out) / 1400.0)
        else:
            eng = nc.vector if e == 'dve' else nc.gpsimd
            P.op(e, lambda: eng.tensor_copy(out=out, in_=in_), r, w,
                 cost=(0.08 + fsz(out) / 960.0) if e == 'dve' else (0.5 + fsz(out) / 400.0))

    for t_, d_, k_ in ((ident, ident_d, 'ident'), (mask, mask_d, 'mask'), (smask, smask_d, 'smask'),
                       (cs, cs_d, 'cs'), (keep, keep_d, 'keep'), (cv, cv_d, 'cv'), (nw, nw_d, 'nw'),
                       (badap, badap_d, 'badap'), (lbr, lbr_d, 'lbr'), (gnw, gnw_d, 'gnw')):
        dma('sp', t_[:], d_, [], [k_])
    P.op('dve', lambda: nc.vector.memset(onesf[:], 1.0), [], ['onesf'])
    P.op('dve', lambda: nc.vector.memset(epsc[:], EPS), [], ['epsc'])
    act(cvs[:], cv[:], AF.Silu, ['cv'], ['cvs'])
    act(cvb[:], cv[:], AF.Silu, ['cv'], ['cvb'])
    P.op('pool', lambda: nc.gpsimd.memset(onesb[:], 1.0), [], ['onesb'])
    for kc in range(8):
        ts('dve', scB[:, kc, :], onesf[:], cvs[:, kc:kc + 1], None, ALU.mult, None, ['onesf', 'cvs'], ['scB'])
    act(lbe[:], lbr[:], AF.Exp, ['lbr'], ['lbe'])
    cp('dve', lbs[:], lbe[:, 0:16], ['lbe'], ['lbs'])
    for l in range(1, DEPTH):
        tt('dve', lbs[:], lbs[:], lbe[:, l * 16:(l + 1) * 16], ALU.add, ['lbs', 'lbe'], ['lbs'])
    P.op('dve', lambda: nc.vector.reciprocal(out=lbs[:], in_=lbs[:]), ['lbs'], ['lbs'])
    P.op('dve', lambda: nc.vector.memset(lbt[:, 0:16], 0.0), [], ['lbt'])
    for l in range(1, DEPTH):
        tt('dve', lbe[:, l * 16:(l + 1) * 16], lbe[:, l * 16:(l + 1) * 16], lbs[:], ALU.mult, ['lbe', 'lbs'], ['lbe'])
        tt('dve', lbt[:, l * 16:(l + 1) * 16], lbt[:, (l - 1) * 16:l * 16], lbe[:, l * 16:(l + 1) * 16], ALU.add,
           ['lbt', 'lbe'], ['lbt'])

    for l in range(DEPTH):
        last = (l == DEPTH - 1)
        with ExitStack() as ps:
            wa = [sb("wa%d" % i, [128, 3 * D], BF16, st=ps) for i in range(2)]
            gb = sb("gb", [128, D], st=ps)
            if os.environ.get("KS_ADA", SCHED_DEFAULT["KS_ADA"]) == "1":
                P.begin_region()
            dma('sp', gb[:], badag_d[l], [], ['gb'])
            for kc in range(8):
                wt = wa[kc % 2]
                wk = 'wa%d' % (kc % 2)
                cdma(wt[:], wada_d[l][kc * 128:(kc + 1) * 128, :], [], [wk])
                for j in range(16):
                    mm(PB[0][:, j:j + 1], wt[:, j * 128:(j + 1) * 128], cvb[:, kc:kc + 1],
                       (kc == 0 and j == 0), (kc == 7 and j == 15), [wk, 'cvb'], [BK[0]])
                for b in range(2):
                    mm(PB[1 + b][:, :], scB[:, kc, :], wt[:, 2048 + b * 512:2048 + (b + 1) * 512],
                       kc == 0, kc == 7, [wk, 'scB'], [BK[1 + b]])
            tt('dve', shsc[:], PB[0][:, 0:16], badap[:, l * 16:(l + 1) * 16], ALU.add, [BK[0], 'badap'], ['shsc'])
            stt(sc1p[:], shsc[:, 8:16], 1.0, nw[:, l * 8:(l + 1) * 8], ALU.add, ALU.mult, ['shsc', 'nw'], ['sc1p'])
            for b in range(2):
                tt('dve', gtB[:, b * 512:(b + 1) * 512], PB[1 + b][:, :], gb[:, b * 512:(b + 1) * 512], ALU.add,
                   [BK[1 + b], 'gb'], ['gtB'])
            if P.pending is not None:
                P.end_region()

        P.barrier()
        if STOP == "ada":
            raise StopIteration
        with ExitStack() as ps:
            xt = [sb("xt%d" % i, [128, D], st=ps) for i in range(2)]
            pt = sb("pt", [128, D], st=ps)
            xn = [sb("xn%d" % i, [128, D], BF16, st=ps) for i in range(2)]
            junk = sb("junk", [128, D], st=ps)
            ss = sb("ss", [128, 1], st=ps)
            ms = sb("ms", [128, 1], st=ps)
            rstd = sb("rstd", [128, 1], st=ps)
            if os.environ.get("KS_A", SCHED_DEFAULT["KS_A"]) == "1":
                P.begin_region()
            for i in range(NT):
                X, xk = xt[i % 2], 'xt%d' % (i % 2)
                XN, xnk = xn[i % 2], 'xn%d' % (i % 2)
                rows = slice(i * 128, (i + 1) * 128)
                if l == 0:
                    dma('sp', X[:], x_d[rows, :], [], [xk])
                    dma('sp', pt[:], pos_d[rows, :], [], ['pt'])
                    tt('dve', X[:], X[:], pt[:], ALU.add, [xk, 'pt'], [xk])
                    dma('sp', xcur_d[rows, :], X[:], [xk], ['xcur%d' % i])
                else:
                    dma('sp', X[:], xcur_d[rows, :], ['xcur%d' % i], [xk])
                act(junk[:], X[:], AF.Square, [xk], ['junk'])
                rsum(ss[:], junk[:], ['junk'], ['ss'])
                ts('dve', ms[:], ss[:], 1.0 / D, EPS, ALU.mult, ALU.add, ['ss'], ['ms'])
                act(ms[:], ms[:], AF.Sqrt, ['ms'], ['ms'])
                recip(rstd[:], ms[:], ['ms'], ['rstd'])
                ts('dve', XN[:], X[:], rstd[:, 0:1], None, ALU.mult, None, [xk, 'rstd'], [xnk])
                if l == 1 and i >= NT - 2:
                    dbg(36 + (i - NT + 2), X[:, 0:512], xk)
                for kc in range(8):
                    tr(PB[7][:, kc * 128:(kc + 1) * 128], XN[:, kc * 128:(kc + 1) * 128], [xnk, 'ident'], [BK[7]])
                for kc in range(8):
                    o_ = hT[:, kc, i * 128:(i + 1) * 128]
                    i_ = PB[7][:, kc * 128:(kc + 1) * 128]
                    act(o_, i_, AF.Identity, [BK[7], 'sc1p', 'shsc'], ['hT'],
                        bias=shsc[:, kc:kc + 1], scale=sc1p[:, kc:kc + 1])

        if P.pending is not None:
            P.end_region()
        if STOP == "A":
            dbg(8, hT[:, 0, 0:512], 'hT', bf=True)
            dbg(0, hT[:, 1, 0:512], 'hT', bf=True)
        P.barrier()
        if STOP == "A":
            raise StopIteration
        with ExitStack() as ps:
            wh = [sb("wh%d" % i, [128, 8, 5, 128], BF16, st=ps) for i in range(2)]
            qbT = sb("qbT", [128, NTOK], BF16, st=ps)
            khb = sb("khb", [128, NT, 128], BF16, st=ps)
            vall = sb("vall", [128, NT, 128], BF16, st=ps)
            oacc = sb("oacc", [128, NTOK], st=ps)
            Dall = sb("Dall", [128, 2, NCH], st=ps)
            qs = sb("qs", [128, 512], st=ps)
            sg = [sb("sg%d" % i, [128, 512], st=ps) for i in range(2)]
            lf = [sb("lf%d" % i, [128, 512], st=ps) for i in range(2)]
            bc = [sb("bc%d" % i, [128, 512], st=ps) for i in range(2)]
            bb = sb("bb", [128, 512], st=ps)
            ea = [sb("ea%d" % i, [128, 512], st=ps) for i in range(2)]
            en = [sb("en%d" % i, [128, 512], st=ps) for i in range(2)]
            lbm = sb("lbm", [128, DEPTH * 16], st=ps)
            ts('dve', lbm[:], lbt[:], -1.0, 1.0, ALU.mult, ALU.add, ['lbt'], ['lbm'])
            er = [sb("er%d" % i, [128, 16], st=ps) for i in range(2)]
            ee = [sb("ee%d" % i, [128, 16], st=ps) for i in range(2)]
            qfT = [sb("qfT%d" % p, [128, 512], BF16, st=ps) for p in range(2)]
            q1 = [[sb("q1%d%d" % (p, i), [128, 512], BF16, st=ps) for i in range(2)] for p in range(2)]
            k1 = [[sb("k1%d%d" % (p, i), [128, 512], BF16, st=ps) for i in range(2)] for p in range(2)]
            q2 = [[sb("q2%d%d" % (p, i), [128, 512], BF16, st=ps) for i in range(2)] for p in range(2)]
            k2 = [[sb("k2%d%d" % (p, i), [128, 512], BF16, st=ps) for i in range(2)] for p in range(2)]
            khT = [sb("khT%d" % i, [128, 512], BF16, st=ps) for i in range(2)]
            khf = [sb("khf%d" % p, [128, 4, 128], BF16, st=ps) for p in range(2)]
            vblk = [sb("vblk%d" % p, [128, 4, 4, 128], BF16, st=ps) for p in range(2)]
            am = [sb("am%d" % p, [128, 512], BF16, st=ps) for p in range(2)]
            NS = 3
            Sf = [[sb("Sf%d%d" % (i, j), [128, 128], st=ps) for j in range(NS)] for i in range(2)]
            Sb = [[sb("Sb%d%d" % (i, j), [128, 128], BF16, st=ps) for j in range(NS)] for i in range(2)]
            szb = sb("szb", [128, 512], BF16, st=ps)
            ot2 = [sb("ot%d" % i, [128, 128], st=ps) for i in range(2)]
            osq2 = [sb("osq%d" % i, [128, 128], BF16, st=ps) for i in range(2)]
            lnm2 = [sb("lnm%d" % i, [128, 128], st=ps) for i in range(2)]
            rs22 = [sb("rs2%d" % i, [128, 128], st=ps) for i in range(2)]
            t12 = [sb("t1%d" % i, [128, 128], st=ps) for i in range(2)]
            yst = [sb("yst%d" % i, [128, 128], BF16, st=ps) for i in range(2)]
            for p in range(2):
                for i in range(2):
                    P.op('pool', lambda: nc.gpsimd.memset(q2[p][i][:], 0.0), [], ['q2%d%d' % (p, i)])
                    P.op('pool', lambda: nc.gpsimd.memset(k2[p][i][:], 0.0), [], ['k2%d%d' % (p, i)])
                P.op('pool', lambda: nc.gpsimd.memset(vblk[p][:], 0.0), [], ['vblk%d' % p])

            def load_wh(h):
                W, wk = wh[h % 2], 'wh%d' % (h % 2)
                for si in range(5):
                    c0 = si * 1024 + h * 128
                    dma('pool', W[:, :, si, :],
                        win_d[l][:, c0:c0 + 128].rearrange("(kc p) c -> p kc c", p=128), [], [wk])

            def v4(ap):
                return ap.rearrange("p (c z t) -> p c z t", z=2, t=16)

            if os.environ.get("KSCHED", "1") == "1":
                P.begin_region()
            load_wh(0)
            for h in range(H):
                W, wk = wh[h % 2], 'wh%d' % (h % 2)
                if h + 1 < H:
                    load_wh(h + 1)
                cur = [0, 0]
                for d in range(2):
                    dma('sp', Sf[d][0][:], s0_d[(l * 2 + d) * H + h], [], ['Sf%d0' % d])
                    cp('act', Sb[d][0][:], Sf[d][0][:], ['Sf%d0' % d], ['Sb%d0' % d])

                def step_state(d, ch, ubank_ap, ubk, seg_end):
                    c = cur[d]
                    n = (c + 1) % NS
                    stt(Sf[d][n][:], Sf[d][c][:], Dall[:, d, ch:ch + 1], ubank_ap, ALU.mult, ALU.add,
                        ['Sf%d%d' % (d, c), 'Dall', ubk], ['Sf%d%d' % (d, n)])
                    if seg_end:
                        seg = ch // 8
                        if seg < NSO:
                            dma('sp', st_d[((seg * DEPTH + l) * 2 + d) * H + h], Sf[d][n][:], ['Sf%d%d' % (d, n)], [])
                        n2 = (n + 1) % NS
                        ts('dve', Sf[d][n2][:], Sf[d][n][:], keep[:, 0:1], None, ALU.mult, None,
                           ['Sf%d%d' % (d, n), 'keep'], ['Sf%d%d' % (d, n2)])
                        n = n2
                    cur[d] = n
                    cp('act', Sb[d][n][:], Sf[d][n][:], ['Sf%d%d' % (d, n)], ['Sb%d%d' % (d, n)])

                def stage_proj(blk):
                    cols = slice(blk * 512, (blk + 1) * 512)
                    for si, bk in ((0, 0), (1, 1), (2, 2)):
                        for kc in range(8):
                            mm(PB[bk][:, :], W[:, kc, si, :], hT[:, kc, cols], kc == 0, kc == 7, [wk, 'hT'], [BK[bk]])
                    for t in range(4):
                        tcg = slice(blk * 512 + t * 128, blk * 512 + (t + 1) * 128)
                        for kc in range(8):
                            mm(PB[4][:, t * 128:(t + 1) * 128], hT[:, kc, tcg], W[:, kc, 3, :], kc == 0, kc == 7,
                               [wk, 'hT'], [BK[4]])

                def elem_dir(blk, d):
                    p = blk % 2
                    cols = slice(blk * 512, (blk + 1) * 512)
                    li = l * 16 + d * 8 + h
                    SG, sgk, LF, lfk, BC, bck = sg[d], 'sg%d' % d, lf[d], 'lf%d' % d, bc[d], 'bc%d' % d
                    EA, eak, EN, enk, ER, erk, EE, eek = ea[d], 'ea%d' % d, en[d], 'en%d' % d, er[d], 'er%d' % d, ee[d], 'ee%d' % d
                    act(SG[:], PB[1 + d][:, :], AF.Sigmoid, [BK[1 + d]], [sgk])
                    yield
                    ts('dve', SG[:], SG[:], lbm[:, li:li + 1], lbt[:, li:li + 1], ALU.mult, ALU.add, [sgk, 'lbm', 'lbt'], [sgk])
                    yield
                    act(LF[:], SG[:], AF.Ln, [sgk], [lfk])
                    yield
                    act(SG[:], SG[:], AF.Identity, [sgk], [sgk], bias=1.0, scale=-1.0)
                    scan(BC[:], smask[:], LF[:], ['smask', lfk], [bck])
                    yield
                    if d == 0:
                        B_, bk_ = BC, bck
                        zn, zf, rpos, epos = 0, 1, 15, 15
                    else:
                        b3 = BC[:].rearrange("p (c t) -> p c t", t=16)
                        bb3 = bb[:].rearrange("p (c t) -> p c t", t=16)
                        tt('dve', bb3, b3[:, :, 15:16].to_broadcast([128, 32, 16]), b3, ALU.subtract, [bck], ['bb'])
                        tt('dve', bb[:], bb[:], LF[:], ALU.add, ['bb', lfk], ['bb'])
                        B_, bk_ = bb, 'bb'
                        zn, zf, rpos, epos = 1, 0, 0, 0
                        yield
                    B4 = v4(B_[:])
                    act(EA[:], B_[:], AF.Exp, [bk_], [eak])
                    act(EN[:], B_[:], AF.Exp, [bk_], [enk], scale=-1.0)
                    act(ER[:], B4[:, :, zn, rpos], AF.Exp, [bk_], [erk])
                    act(EE[:], B4[:, :, zf, epos], AF.Exp, [bk_], [eek])
                    yield
                    tt('dve', EA[:], qs[:], EA[:], ALU.mult, ['qs', eak], [eak])
                    tt('dve', EN[:], SG[:], EN[:], ALU.mult, [sgk, enk], [enk])
                    tt('dve', Dall[:, d, blk * 16:(blk + 1) * 16], ER[:], EE[:], ALU.mult, [erk, eek], ['Dall'])
                    yield
                    cp('act', q1[p][d][:], EA[:], [eak], ['q1%d%d' % (p, d)])
                    cp('act', k1[p][d][:], EN[:], [enk], ['k1%d%d' % (p, d)])
                    er_b = ER[:].unsqueeze(2).to_broadcast([128, 16, 16])
                    ee_b = EE[:].unsqueeze(2).to_broadcast([128, 16, 16])
                    D_b = Dall[:, d, blk * 16:(blk + 1) * 16].unsqueeze(2).to_broadcast([128, 16, 16])
                    q1f4, k1f4 = v4(EA[:]), v4(EN[:])
                    if d == 0:
                        QF4, qfk = v4(qfT[p][:]), 'qfT%d' % p
                    else:
                        QF4, qfk = v4(qbT[:, cols]), 'qbT'
                    KH4 = v4(khT[d][:])
                    tt('dve', KH4[:, :, zn, :], k1f4[:, :, zn, :], D_b, ALU.mult, [enk, 'Dall'], ['khT%d' % d])
                    tt('dve', KH4[:, :, zf, :], k1f4[:, :, zf, :], ee_b, ALU.mult, [enk, eek], ['khT%d' % d])
                    yield
                    for t in range(4):
                        tr(PB[7][:, (d * 4 + t) * 128:(d * 4 + t + 1) * 128], khT[d][:, t * 128:(t + 1) * 128],
                           ['khT%d' % d, 'ident'], [BK[7]])
                    tt('dve', QF4[:, :, zf, :], q1f4[:, :, zf, :], er_b, ALU.mult, [eak, erk], [qfk])
                    cp('act', QF4[:, :, zn, :], q1f4[:, :, zn, :], [eak], [qfk])
                    yield
                    cp('act', v4(q2[p][d][:])[:, :, zf, :], q1f4[:, :, zf, :], [eak], ['q2%d%d' % (p, d)])
                    tt('dve', v4(k2[p][d][:])[:, :, zn, :], k1f4[:, :, zn, :], er_b, ALU.mult, [enk, erk],
                       ['k2%d%d' % (p, d)])
                    yield

                def stage_elem(blk):
                    p = blk % 2
                    act(qs[:], PB[0][:, :], AF.Sigmoid, [BK[0]], ['qs'])
                    cp('dve', vall[:, blk * 4:(blk + 1) * 4, :], PB[4][:, :].rearrange("p (t v) -> p t v", t=4),
                       [BK[4]], ['vall'])
                    tt('dve', qs[:], PB[0][:, :], qs[:], ALU.mult, [BK[0], 'qs'], ['qs'])
                    gens = [elem_dir(blk, 0), elem_dir(blk, 1)]
                    alive = [True, True]
                    n_ = 0
                    while any(alive):
                        for d in range(2):
                            if alive[d]:
                                try:
                                    next(gens[d])
                                except StopIteration:
                                    alive[d] = False
                        n_ += 1
                        if n_ == 1:
                            if blk + 1 < NB:
                                stage_proj(blk + 1)
                            for j in range(4):
                                dma('sp', vblk[p][32 * j:32 * j + 32, :, j, :],
                                    vall[32 * j:32 * j + 32, blk * 4:(blk + 1) * 4, :], ['vall'], ['vblk%d' % p])
                        if n_ % 2 == 0:
                            yield
                    for t in range(4):
                        cp('act', khf[p][:, t, :], PB[7][:, t * 128:(t + 1) * 128], [BK[7]], ['khf%d' % p])
                        cp('act', khb[:, blk * 4 + t, :], PB[7][:, 512 + t * 128:512 + (t + 1) * 128], [BK[7]], ['khb'])

                def stage_scan(blk):
                    p = blk % 2
                    for t in range(4):
                        tile = blk * 4 + t
                        tcl = slice(t * 128, (t + 1) * 128)
                        tcg = slice(tile * 128, (tile + 1) * 128)
                        A, ak = am[tile % 2], 'am%d' % (tile % 2)
                        for d in range(2):
                            mm(PB[5][:, (2 * d) * 128:(2 * d + 1) * 128], k1[p][d][:, tcl], q1[p][d][:, tcl], True, True,
                               ['k1%d%d' % (p, d), 'q1%d%d' % (p, d)], [BK[5]])
                            mm(PB[5][:, (2 * d + 1) * 128:(2 * d + 2) * 128], k2[p][d][:, tcl], q2[p][d][:, tcl], True, True,
                               ['k2%d%d' % (p, d), 'q2%d%d' % (p, d)], [BK[5]])
                        tt('dve', A[:], PB[5][:, :], mask[:], ALU.mult, [BK[5], 'mask'], [ak])
                        mm(PB[3][:, :], khf[p][:, t, :], vblk[p][:, t, :, :].rearrange("p j v -> p (j v)"), True, True,
                           ['khf%d' % p, 'vblk%d' % p], [BK[3]])
                        for i4 in range(4):
                            mm(PB[6][:, 0:128], vall[:, tile, :], A[:, i4 * 128:(i4 + 1) * 128], i4 == 0, False,
                               ['vall', ak], [BK[6]])
                        for j in range(4):
                            ch = tile * 4 + j
                            c = cur[0]
                            mm(PB[6][:, 32 * j:32 * j + 32], Sb[0][c][:], qfT[p][:, t * 128 + 32 * j:t * 128 + 32 * j + 32],
                               False, j == 3, ['Sb0%d' % c, 'qfT%d' % p], [BK[6]])
                            step_state(0, ch, PB[3][:, j * 128:(j + 1) * 128], BK[3], ch % 8 == 7)
                        cp('act', oacc[:, tcg], PB[6][:, 0:128], [BK[6]], ['oacc'])
                        yield

                stage_proj(0)
                for _ in stage_elem(0):
                    pass
                for blk in range(NB):
                    gs = stage_scan(blk)
                    if blk + 1 < NB:
                        ge = stage_elem(blk + 1)
                    else:
                        ge = iter(())
                    done_s = done_e = False
                    while not (done_s and done_e):
                        if not done_s:
                            try:
                                next(gs)
                            except StopIteration:
                                done_s = True
                        if not done_e:
                            try:
                                next(ge)
                            except StopIteration:
                                done_e = True
                if STOP == "B3":
                    raise StopIteration
                szs = [(q1[pp][dd], 'q1%d%d' % (pp, dd)) for pp in range(2) for dd in range(2)] + \
                      [(k1[pp][dd], 'k1%d%d' % (pp, dd)) for pp in range(2) for dd in range(2)]
                for blk in range(NB):
                    cols = slice(blk * 512, (blk + 1) * 512)
                    bk = blk % 3
                    for kc in range(8):
                        mm(PB[bk][:, :], W[:, kc, 4, :], hT[:, kc, cols], kc == 0, kc == 7, [wk, 'hT'], [BK[bk]])
                    act(qs[:], PB[bk][:, :], AF.Sigmoid, [BK[bk]], ['qs'])
                    tt('dve', szs[blk][0][:], PB[bk][:, :], qs[:], ALU.mult, [BK[bk], 'qs'], [szs[blk][1]])

                def fin1(tile):
                    q_ = tile % 2
                    tcg = slice(tile * 128, (tile + 1) * 128)
                    tt('dve', ot2[q_][:], oacc[:, tcg], PB[6][:, q_ * 128:(q_ + 1) * 128], ALU.add, ['oacc', BK[6]], ['ot%d' % q_])
                    act(osq2[q_][:], ot2[q_][:], AF.Square, ['ot%d' % q_], ['osq%d' % q_])
                    mm(PB[5][:, q_ * 128:(q_ + 1) * 128], onesb[:], osq2[q_][:], True, True, ['onesb', 'osq%d' % q_], [BK[5]])
                    act(lnm2[q_][:], PB[5][:, q_ * 128:(q_ + 1) * 128], AF.Ln, [BK[5], 'epsc'], ['lnm%d' % q_],
                        bias=epsc[:, 0:1], scale=1.0 / 128)
                    act(rs22[q_][:], lnm2[q_][:], AF.Exp, ['lnm%d' % q_], ['rs2%d' % q_], scale=-0.5)

                def fin2(tile):
                    q_ = tile % 2
                    blk, t = divmod(tile, 4)
                    tcg = slice(tile * 128, (tile + 1) * 128)
                    tt('dve', t12[q_][:], ot2[q_][:], rs22[q_][:], ALU.mult, ['ot%d' % q_, 'rs2%d' % q_], ['t1%d' % q_])
                    Y, yk = yst[q_], 'yst%d' % q_
                    stt(Y[:], t12[q_][:], gnw[:, l * 8 + h:l * 8 + h + 1], szs[blk][0][:, t * 128:(t + 1) * 128],
                        ALU.mult, ALU.mult, ['t1%d' % q_, 'gnw', szs[blk][1]], [yk])
                    dma('sp', ya_d[h][:, tcg], Y[:], [yk], ['ya_%d_%d' % (h, blk)])

                prev_tile = None
                for tile in reversed(range(NT)):
                    blk, t = divmod(tile, 4)
                    q_ = tile % 2
                    if t == 3:
                        vb, vbk = vblk[blk % 2], 'vblk%d' % (blk % 2)
                        for j in range(4):
                            dma('sp', vb[32 * j:32 * j + 32, :, j, :], vall[32 * j:32 * j + 32, blk * 4:(blk + 1) * 4, :],
                                ['vall'], [vbk])
                    mm(PB[3][:, :], khb[:, tile, :], vb[:, t, :, :].rearrange("p j v -> p (j v)"), True, True,
                       ['khb', vbk], [BK[3]])
                    first = True
                    for j in (3, 2, 1, 0):
                        ch = tile * 4 + j
                        c = cur[1]
                        mm(PB[6][:, q_ * 128 + 32 * j:q_ * 128 + 32 * j + 32], Sb[1][c][:],
                           qbT[:, tile * 128 + 32 * j:tile * 128 + 32 * j + 32],
                           first, j == 0, ['Sb1%d' % c, 'qbT'], [BK[6]])
                        first = False
                        step_state(1, ch, PB[3][:, j * 128:(j + 1) * 128], BK[3], ch % 8 == 0)
                    if prev_tile is not None:
                        fin2(prev_tile)
                    fin1(tile)
                    prev_tile = tile
                fin2(prev_tile)
            if P.pending is not None:
                P.end_region()

        P.barrier()
        if STOP == "B":
            raise StopIteration
        with ExitStack() as ps:
            ucs = sb("ucs", [128, NT, 4, 256], BF16, st=ps)
            szB = sb("szB", [128, 4, NTOK], BF16, st=ps)
            wu = [sb("wu%d" % i, [128, 8, 128], BF16, st=ps) for i in range(2)]
            wz = [sb("wz%d" % i, [128, 8, 128], BF16, st=ps) for i in range(2)]
            uT = sb("uT", [128, 512], BF16, st=ps)
            NTB = 6
            tb = [sb("tb%d" % i, [128, 1024], BF16, st=ps) for i in range(NTB)]
            ybs = [sb("ybs%d" % i, [128, 512], BF16, st=ps) for i in range(2)]
            if os.environ.get("KS_C", SCHED_DEFAULT["KS_C"]) == "1":
                P.begin_region()
            for g in range(4):
                dma('pool', wu[g % 2][:], win_d[l][:, 5120 + g * 128:5120 + (g + 1) * 128].rearrange(
                    "(kc p) c -> p kc c", p=128), [], ['wu%d' % (g % 2)])
                dma('pool', wz[g % 2][:], win_d[l][:, 5632 + g * 128:5632 + (g + 1) * 128].rearrange(
                    "(kc p) c -> p kc c", p=128), [], ['wz%d' % (g % 2)])
                for blk in range(NB):
                    cols = slice(blk * 512, (blk + 1) * 512)
                    for kc in range(8):
                        mm(PB[4][:, :], wu[g % 2][:, kc, :], hT[:, kc, cols], kc == 0, kc == 7,
                           ['wu%d' % (g % 2), 'hT'], [BK[4]])
                    cp('act', uT[:], PB[4][:, :], [BK[4]], ['uT'])
                    for tp in range(2):
                        for t2 in range(2):
                            t = tp * 2 + t2
                            mm(PB[5][:, t2 * 256:(t2 + 1) * 256], uT[:, t * 128:(t + 1) * 128], cs[:], True, True,
                               ['uT', 'cs'], [BK[5]])
                        cp('dve', ucs[:, blk * 4 + tp * 2:blk * 4 + tp * 2 + 2, g, :],
                           PB[5][:, :].rearrange("p (t c) -> p t c", t=2), [BK[5]], ['ucs'])
                    for kc in range(8):
                        mm(PB[6][:, :], wz[g % 2][:, kc, :], hT[:, kc, cols], kc == 0, kc == 7,
                           ['wz%d' % (g % 2), 'hT'], [BK[6]])
                    act(szB[:, g, cols], PB[6][:, :], AF.Silu, [BK[6]], ['szB'])
            n_t = 0
            for kb in range(NB):
                kcols = slice(kb * 512, (kb + 1) * 512)
                for a in range(NT):
                    T_, tk = tb[n_t % NTB], 'tb%d' % (n_t % NTB)
                    n_t += 1
                    dma('sp', T_[:], tbl_d[kb, a], [], [tk])
                    for g in range(4):
                        mm(PB[g][:, :], ucs[:, a, g, 0:128], T_[:, 0:512], a == 0, False, ['ucs', tk], [BK[g]])
                        mm(PB[g][:, :], ucs[:, a, g, 128:256], T_[:, 512:1024], False, a == NT - 1, ['ucs', tk], [BK[g]])
                for g in range(4):
                    Yb, ybk = ybs[g % 2], 'ybs%d' % (g % 2)
                    tt('dve', Yb[:], PB[g][:, :], szB[:, g, kcols], ALU.mult, [BK[g], 'szB'], [ybk])
                    dma('sp', yb_d[g][:, kcols], Yb[:], [ybk], ['yb_%d_%d' % (g, kb)])
            if P.pending is not None:
                P.end_region()

        P.barrier()
        if STOP == "C":
            raise StopIteration
        with ExitStack() as ps:
            wpa = sb("wpa", [128, 8, D], BF16, st=ps)
            wpb = sb("wpb", [128, 4, D], BF16, st=ps)
            wo = sb("wo", [128, 8, D], BF16, st=ps)
            wg = sb("wg", [128, 8, 2 * D], BF16, st=ps)
            ya = sb("ya", [128, 8, 512], BF16, st=ps)
            yb = sb("yb", [128, 4, 512], BF16, st=ps)
            mg = sb("mg", [128, 8, 512], BF16, st=ps)
            sA = sb("sA", [128, 512], st=ps)
            sB_ = sb("sB", [128, 512], st=ps)
            m1 = sb("m1", [128, 512], st=ps)
            m2 = sb("m2", [128, 512], st=ps)
            xt = [sb("dxt%d" % i, [128, D], st=ps) for i in range(2)]
            xw = [sb("dxw%d" % i, [128, D], st=ps) for i in range(2)]
            tmp = sb("dtmp", [128, D], st=ps)
            fnw = sb("fnw", [128, D], st=ps)
            junk = sb("djunk", [128, D], st=ps)
            ss = sb("dss", [128, 1], st=ps)
            ms = sb("dms", [128, 1], st=ps)
            rstd = sb("drstd", [128, 1], st=ps)
            if os.environ.get("KS_D", SCHED_DEFAULT["KS_D"]) == "1":
                P.begin_region()
            for kc in range(8):
                rws = slice(kc * 128, (kc + 1) * 128)
                cdma(wpa[:, kc, :], wpa_d[l][rws, :], [], ['wpa'])
                cdma(wo[:, kc, :], wo_d[l][rws, :], [], ['wo'])
                cdma(wg[:, kc, :], win_d[l][rws, 6144:8192], [], ['wg'])
                if kc < 4:
                    cdma(wpb[:, kc, :], wpb_d[l][rws, :], [], ['wpb'])
            if last:
                dma('sp', fnw[:], fnw_d, [], ['fnw'])
            for blk in range(NB):
                cols = slice(blk * 512, (blk + 1) * 512)
                dma('sp', ya[:], ya_d[:, :, cols].rearrange("h p n -> p h n"),
                    ['ya_%d_%d' % (h, blk) for h in range(H)], ['ya'])
                dma('sp', yb[:], yb_d[:, :, cols].rearrange("g p n -> p g n"),
                    ['yb_%d_%d' % (g, blk) for g in range(4)], ['yb'])
                if blk == 0 and l == 0:
                    for hh in range(8):
                        dbg(24 + hh, ya[:, hh, :], 'ya', bf=True)
                    for g in range(4):
                        dbg(32 + g, yb[:, g, :], 'yb', bf=True)
                for dc in range(8):
                    dcs = slice(dc * 128, (dc + 1) * 128)
                    ba = 0 if dc % 2 == 0 else 6
                    for hh in range(8):
                        mm(PB[ba][:, :], wpa[:, hh, dcs], ya[:, hh, :], hh == 0, hh == 7, ['wpa', 'ya'], [BK[ba]])
                    for g in range(4):
                        mm(PB[1][:, :], wpb[:, g, dcs], yb[:, g, :], g == 0, g == 3, ['wpb', 'yb'], [BK[1]])
                    for kc in range(8):
                        mm(PB[2][:, :], wg[:, kc, dcs], hT[:, kc, cols], kc == 0, kc == 7, ['wg', 'hT'], [BK[2]])
                    for kc in range(8):
                        mm(PB[3][:, :], wg[:, kc, D + dc * 128:D + (dc + 1) * 128], hT[:, kc, cols], kc == 0, kc == 7,
                           ['wg', 'hT'], [BK[3]])
                    act(sA[:], PB[2][:, :], AF.Sigmoid, [BK[2]], ['sA'])
                    act(sB_[:], PB[3][:, :], AF.Sigmoid, [BK[3]], ['sB'])
                    tt('dve', m1[:], PB[ba][:, :], sA[:], ALU.mult, [BK[ba], 'sA'], ['m1'])
                    tt('dve', m2[:], PB[1][:, :], sB_[:], ALU.mult, [BK[1], 'sB'], ['m2'])
                    tt('pool', mg[:, dc, :], m1[:], m2[:], ALU.add, ['m1', 'm2'], ['mg'])
                for t in range(4):
                    tile = blk * 4 + t
                    rows = slice(tile * 128, (tile + 1) * 128)
                    X, xk = xt[tile % 2], 'dxt%d' % (tile % 2)
                    XW, xwk = xw[tile % 2], 'dxw%d' % (tile % 2)
                    dma('sp', X[:], xcur_d[rows, :], ['xcur%d' % tile], [xk])
                    for half in range(2):
                        hc = slice(half * 512, (half + 1) * 512)
                        bk = 4 + half
                        for dc in range(8):
                            mm(PB[bk][:, :], mg[:, dc, t * 128:(t + 1) * 128], wo[:, dc, hc], dc == 0, dc == 7,
                               ['mg', 'wo'], [BK[bk]])
                        tt('dve', tmp[:, hc], PB[bk][:, :], gtB[:, hc], ALU.mult, [BK[bk], 'gtB'], ['dtmp'])
                        tt('pool', XW[:, hc], tmp[:, hc], X[:, hc], ALU.add, ['dtmp', xk], [xwk])
                    if not last:
                        dma('sp', xcur_d[rows, :], XW[:], [xwk], ['xcur%d' % tile])
                    else:
                        act(junk[:], XW[:], AF.Square, [xwk], ['djunk'])
                        rsum(ss[:], junk[:], ['djunk'], ['dss'])
                        ts('dve', ms[:], ss[:], 1.0 / D, EPS, ALU.mult, ALU.add, ['dss'], ['dms'])
                        act(ms[:], ms[:], AF.Sqrt, ['dms'], ['dms'])
                        recip(rstd[:], ms[:], ['dms'], ['drstd'])
                        stt(tmp[:], XW[:], rstd[:, 0:1], fnw[:], ALU.mult, ALU.mult, [xwk, 'drstd', 'fnw'], ['dtmp'])
                        dma('sp', y_d[rows, :], tmp[:], ['dtmp'], [])
            if P.pending is not None:
                P.end_region()
        P.barrier()
    es.close()


_CACHE = {}


def get_nc(NT, DEPTH, NSO):
    key = (NT, DEPTH, NSO)
    if key not in _CACHE:
        nc1 = bass.Bass("TRN2", target_bir_lowering=False)
        P1 = Prog(nc1, None, None)
        build(nc1, P1, NT, DEPTH, NSO)
        nc2 = bass.Bass("TRN2", target_bir_lowering=False)
        es = ExitStack()
        P2 = Prog(nc2, es, P1.need)
        build(nc2, P2, NT, DEPTH, NSO)
        es.close()
        _CACHE[key] = nc2
    return _CACHE[key]


def consts(NT):
    bf = ml_dtypes.bfloat16
    c = {}
    c["ident"] = np.eye(128, dtype=np.float32).astype(bf)
    s = np.arange(128)[:, None]
    t = np.arange(128)[None, :]
    same = (s // 32) == (t // 32)
    hs, ht = (s % 32) // 16, (t % 32) // 16
    c["mask"] = np.concatenate([same & (hs == ht) & (s <= t), same & (hs == 0) & (ht == 1),
                                same & (hs == ht) & (s >= t), same & (hs == 1) & (ht == 0)], axis=1).astype(np.float32)
    sm = np.ones((128, 512), np.float32)
    sm[:, ::16] = 0.0
    c["smask"] = sm
    j = np.arange(128)
    ang = 2 * np.pi * np.outer(j, j) / 128.0
    c["cs"] = (np.concatenate([np.cos(ang), -np.sin(ang)], axis=1) / np.sqrt(128.0)).astype(np.float32).astype(bf)
    return c


def dft_table(NT, L):
    NTOK = NT * 128
    NB = NT // 4
    n = np.arange(NTOK)
    tblc = np.zeros((NTOK, NTOK), np.float32)
    tbls = np.zeros((NTOK, NTOK), np.float32)
    for s0 in range(0, NTOK, L):
        m = np.arange(L)
        ang = 2 * np.pi * (np.outer(m, m) % L) / L
        tblc[s0:s0 + L, s0:s0 + L] = np.cos(ang) / np.sqrt(L)
        tbls[s0:s0 + L, s0:s0 + L] = np.sin(ang) / np.sqrt(L)
    out = np.zeros((NB, NT, 128, 1024), ml_dtypes.bfloat16)
    for kb in range(NB):
        cc = tblc[:, kb * 512:(kb + 1) * 512].reshape(NT, 128, 512)
        sn = tbls[:, kb * 512:(kb + 1) * 512].reshape(NT, 128, 512)
        out[kb, :, :, 0:512] = cc.astype(ml_dtypes.bfloat16)
        out[kb, :, :, 512:1024] = sn.astype(ml_dtypes.bfloat16)
    return out


def pos_table(L):
    GRID_W = 64
    rows = L // GRID_W
    r = np.repeat(np.arange(rows, dtype=np.float32), GRID_W)
    col = np.tile(np.arange(GRID_W, dtype=np.float32), rows)
    nf = D // 4
    freqs = (1.0 / (np.float32(10000.0) ** (np.arange(nf, dtype=np.float32) / np.float32(nf)))).astype(np.float32)

    def emb(p):
        a = (p[:, None] * freqs[None, :]).astype(np.float32)
        return np.concatenate([np.sin(a), np.cos(a)], axis=-1)
    return np.concatenate([emb(r), emb(col)], axis=-1).astype(np.float32)


def shared_inputs(DEPTH, norm_w, w_ada, b_ada, w_in, lb_raw, gnorm_w, w_pa, w_pb, w_o, final_norm_w):
    f = np.float32
    m = {}
    m["nw"] = np.ascontiguousarray(np.asarray(norm_w, f).reshape(DEPTH, 8, 128).transpose(2, 0, 1).reshape(128, DEPTH * 8))
    m["w_ada"] = np.ascontiguousarray(np.asarray(w_ada, f))
    ba = np.asarray(b_ada, f)
    m["bada_p"] = np.ascontiguousarray(ba[:, :2048].reshape(DEPTH, 16, 128).transpose(2, 0, 1).reshape(128, DEPTH * 16))
    m["bada_g"] = np.ascontiguousarray(np.broadcast_to(ba[:, None, 2048:], (DEPTH, 128, D)))
    m["w_in"] = np.ascontiguousarray(np.asarray(w_in, f))
    m["lbr"] = np.ascontiguousarray(np.asarray(lb_raw, f).reshape(DEPTH, 2, 8, 128).transpose(3, 0, 1, 2).reshape(128, DEPTH * 16))
    m["gnw"] = np.ascontiguousarray(np.asarray(gnorm_w, f).reshape(DEPTH, 8, 128).transpose(2, 0, 1).reshape(128, DEPTH * 8))
    m["w_pa"] = np.ascontiguousarray(np.asarray(w_pa, f))
    m["w_pb"] = np.ascontiguousarray(np.asarray(w_pb, f))
    m["w_o"] = np.ascontiguousarray(np.asarray(w_o, f))
    m["fnw"] = np.ascontiguousarray(np.broadcast_to(np.asarray(final_norm_w, f)[None, :], (128, D)))
    return m


def core_inputs(shared, cst, x, pos, cvec, s0, keepv, tbl):
    m = dict(shared)
    m.update(cst)
    m["x"] = np.ascontiguousarray(x, dtype=np.float32)
    m["pos"] = np.ascontiguousarray(pos, dtype=np.float32)
    m["cv"] = np.ascontiguousarray(np.asarray(cvec, np.float32).reshape(8, 128).T)
    m["s0"] = np.ascontiguousarray(np.asarray(s0, np.float32).reshape(-1, 128, 128))
    m["keep"] = np.full((128, 1), keepv, np.float32)
    m["tbl"] = tbl
    return m


def kernel(x_prompt, x_sample, state_hgrn, c, c_ctx, norm_w, w_ada, b_ada, w_in, lb_raw,
           gnorm_w, w_pa, w_pb, w_o, final_norm_w):
    DEPTH, NT, NSO = 4, 32, 8
    NTOK = NT * 128
    x_prompt = np.asarray(x_prompt, np.float32)
    x_sample = np.asarray(x_sample, np.float32)
    state_hgrn = np.asarray(state_hgrn, np.float32)
    c = np.asarray(c, np.float32)
    c_ctx = np.asarray(c_ctx, np.float32)
    shared = shared_inputs(DEPTH, norm_w, w_ada, b_ada, w_in, lb_raw, gnorm_w, w_pa, w_pb, w_o, final_norm_w)
    cst = consts(NT)
    tbl_s = dft_table(NT, NTOK)
    tbl_p = dft_table(NT, SEG)
    pos = pos_table(NTOK)
    zpos = np.zeros((NTOK, D), np.float32)
    zs0 = np.zeros((DEPTH * 2 * H, 128, 128), np.float32)
    in_maps = []
    for b in range(4):
        in_maps.append(core_inputs(shared, cst, x_sample[b], pos, c[b], state_hgrn[b], 1.0, tbl_s))
    for k in range(4):
        xp = np.zeros((NTOK, D), np.float32)
        xp[:8 * SEG] = x_prompt[8 * k:8 * k + 8].reshape(8 * SEG, D)
        in_maps.append(core_inputs(shared, cst, xp, zpos, c_ctx, zs0, 0.0, tbl_p))
    nc = get_nc(NT, DEPTH, NSO)
    res = run_bass_kernel_spmd(nc, in_maps, core_ids=list(range(8)))
    R = res.results
    y_sample = np.stack([np.asarray(R[b]["y"], np.float32) for b in range(4)], axis=0)
    y_prompt = np.concatenate([np.asarray(R[4 + k]["y"], np.float32)[:8 * SEG].reshape(8, SEG, D) for k in range(4)], axis=0)
    new_state = np.concatenate([np.asarray(R[4 + k]["st"], np.float32).reshape(NSO, DEPTH, 2, H, 128, 128)
                                for k in range(4)], axis=0)
    return (y_prompt, y_sample, new_state)
```
